# Optimizing a Trainium2 kernel written in Bass

```python
import math
import jax, jax.numpy as jnp
from jax import lax
import numpy as np

D_MODEL = 1024
BATCH = 16
SEQ = 4096
DEPTH = 4
DEC_BATCH = 8
DEC_SEQ = 2048
PAST_LEN = 128

N_MIXERS = 2
N_ATTN_LAYERS = (DEPTH + 1) // 2
N_CONV_LAYERS = DEPTH // 2
HEAD_DIM = 64
N_HEADS = D_MODEL // HEAD_DIM
N_KV_HEADS = N_HEADS // 4
Q_PER_KV = N_HEADS // N_KV_HEADS
WINDOW = 128
BLOCK = 128
N_BUCKETS = 32
MAX_DISTANCE = 128
CONV_WIDTH = 3
D_FF = 4 * D_MODEL
PLE_DIM = 256
EPS = 1e-6
NEG_INF = -1e30

kernel_name = "hybrid_swa_shortconv_encoder"


def rms_norm(x, g):
    xf = x.astype(jnp.float32)
    y = xf * lax.rsqrt(jnp.mean(xf * xf, axis=-1, keepdims=True) + EPS)
    return (y * g.astype(jnp.float32)).astype(x.dtype)


def t5_bucket(rel):
    half = N_BUCKETS // 2
    max_exact = half // 2
    ret = jnp.where(rel > 0, half, 0)
    n = jnp.abs(rel)
    nf = jnp.maximum(n, 1).astype(jnp.float32)
    large = max_exact + (jnp.log(nf / max_exact) / math.log(MAX_DISTANCE / max_exact)
                         * (half - max_exact)).astype(jnp.int32)
    large = jnp.minimum(large, half - 1)
    return ret + jnp.where(n < max_exact, n, large)


def band_geometry(rel_bias):
    rel = (jnp.arange(3 * BLOCK)[None, :] - BLOCK) - jnp.arange(BLOCK)[:, None]
    in_window = jnp.abs(rel) <= WINDOW
    bias = rel_bias[t5_bucket(rel)].astype(jnp.float32)
    bias = jnp.transpose(bias, (2, 0, 1)).reshape(N_KV_HEADS, Q_PER_KV, BLOCK, 3 * BLOCK)
    return bias, in_window


def window_attention(h, w_qkv, w_o, sink, band_bias, in_window):
    b, s, _ = h.shape
    nb = s // BLOCK
    qkv = h @ w_qkv
    nq = N_HEADS * HEAD_DIM
    nk = N_KV_HEADS * HEAD_DIM
    q = qkv[..., :nq].reshape(b, nb, BLOCK, N_KV_HEADS, Q_PER_KV, HEAD_DIM)
    k = qkv[..., nq:nq + nk].reshape(b, s, N_KV_HEADS, HEAD_DIM)
    v = qkv[..., nq + nk:].reshape(b, s, N_KV_HEADS, HEAD_DIM)
    pad = ((0, 0), (BLOCK, BLOCK), (0, 0), (0, 0))
    kp = jnp.pad(k, pad)
    vp = jnp.pad(v, pad)
    sink_logit = sink.astype(jnp.float32).reshape(1, N_KV_HEADS, Q_PER_KV, 1, 1)
    scale = HEAD_DIM ** -0.5

    def block_fn(n):
        qb = lax.dynamic_index_in_dim(q, n, axis=1, keepdims=False)
        kb = lax.dynamic_slice_in_dim(kp, n * BLOCK, 3 * BLOCK, axis=1)
        vb = lax.dynamic_slice_in_dim(vp, n * BLOCK, 3 * BLOCK, axis=1)
        key_pos = n * BLOCK - BLOCK + jnp.arange(3 * BLOCK)
        valid = in_window & ((key_pos >= 0) & (key_pos < s))[None, :]
        logits = jnp.einsum('bqgrd,bkgd->bgrqk', qb, kb,
                            preferred_element_type=jnp.float32) * scale + band_bias
        logits = jnp.where(valid, logits, NEG_INF)
        m = jnp.maximum(jnp.max(logits, axis=-1, keepdims=True), sink_logit)
        pexp = jnp.exp(logits - m)
        denom = jnp.sum(pexp, axis=-1, keepdims=True) + jnp.exp(sink_logit - m)
        probs = (pexp / denom).astype(vb.dtype)
        return jnp.einsum('bgrqk,bkgd->bqgrd', probs, vb)

    out = lax.map(block_fn, jnp.arange(nb))
    out = jnp.moveaxis(out, 0, 1).reshape(b, s, nq)
    return out @ w_o


def short_conv(h, w_in, conv_w, w_out):
    s = h.shape[1]
    b_gate, c_gate, u = jnp.split(h @ w_in, 3, axis=-1)
    z = c_gate * u
    half = CONV_WIDTH // 2
    zp = jnp.pad(z, ((0, 0), (half, half), (0, 0)))
    conv = zp[:, 0:s] * conv_w[0]
    for t in range(1, CONV_WIDTH):
        conv = conv + zp[:, t:t + s] * conv_w[t]
    return (b_gate * conv) @ w_out


def trunk(x, p, rel_bias, attn_w_qkv, attn_w_o, attn_sink, conv_w_in, conv_w, conv_w_out,
          mlp_w_up, mlp_w_down, ple_w_gate, ple_w_proj, norm_mix, norm_mlp, norm_ple, final_norm):
    band_bias, in_window = band_geometry(rel_bias)
    for i in range(DEPTH):
        j = i // N_MIXERS
        h = rms_norm(x, norm_mix[i])
        if i % N_MIXERS == 0:
            x = x + window_attention(h, attn_w_qkv[j], attn_w_o[j], attn_sink[j], band_bias, in_window)
        else:
            x = x + short_conv(h, conv_w_in[j], conv_w[j], conv_w_out[j])
        h = rms_norm(x, norm_mlp[i])
        x = x + jnp.square(jax.nn.relu(h @ mlp_w_up[i])) @ mlp_w_down[i]
        gate = jax.nn.sigmoid(rms_norm(x, norm_ple[i]) @ ple_w_gate[i])
        x = x + (p[i] @ ple_w_proj[i]) * gate
    return rms_norm(x, final_norm)


def setup_inputs(seed: int = 0) -> dict:
    key = jax.random.key(seed)
    ks = jax.random.split(key, 20)
    f32 = jnp.float32
    D = D_MODEL
    qkv_w = (N_HEADS + 2 * N_KV_HEADS) * HEAD_DIM

    def nrm(k, shape, scale):
        return jax.random.normal(k, shape, f32) * scale

    return {
        "x_prompt": nrm(ks[0], (BATCH, SEQ, D), 1.0),
        "x_sample": nrm(ks[1], (DEC_BATCH, DEC_SEQ, D), 1.0),
        "p_prompt": nrm(ks[2], (DEPTH, BATCH, SEQ, PLE_DIM), 1.0),
        "p_sample": nrm(ks[3], (DEPTH, DEC_BATCH, DEC_SEQ, PLE_DIM), 1.0),
        "rel_bias": nrm(ks[4], (N_BUCKETS, N_HEADS), 0.3),
        "attn_w_qkv": nrm(ks[5], (N_ATTN_LAYERS, D, qkv_w), D ** -0.5),
        "attn_w_o": nrm(ks[6], (N_ATTN_LAYERS, N_HEADS * HEAD_DIM, D), (N_HEADS * HEAD_DIM) ** -0.5),
        "attn_sink": nrm(ks[7], (N_ATTN_LAYERS, N_HEADS), 0.5),
        "conv_w_in": nrm(ks[8], (N_CONV_LAYERS, D, 3 * D), D ** -0.5),
        "conv_w": nrm(ks[9], (N_CONV_LAYERS, CONV_WIDTH, D), CONV_WIDTH ** -0.5),
        "conv_w_out": nrm(ks[10], (N_CONV_LAYERS, D, D), D ** -0.5),
        "mlp_w_up": nrm(ks[11], (DEPTH, D, D_FF), D ** -0.5),
        "mlp_w_down": nrm(ks[12], (DEPTH, D_FF, D), D_FF ** -0.5),
        "ple_w_gate": nrm(ks[13], (DEPTH, D, D), D ** -0.5),
        "ple_w_proj": nrm(ks[14], (DEPTH, PLE_DIM, D), PLE_DIM ** -0.5),
        "norm_mix": 1.0 + nrm(ks[15], (DEPTH, D), 0.05),
        "norm_mlp": 1.0 + nrm(ks[16], (DEPTH, D), 0.05),
        "norm_ple": 1.0 + nrm(ks[17], (DEPTH, D), 0.05),
        "final_norm": 1.0 + nrm(ks[18], (D,), 0.05),
    }


def reference(x_prompt, x_sample, p_prompt, p_sample, rel_bias, attn_w_qkv, attn_w_o, attn_sink,
              conv_w_in, conv_w, conv_w_out, mlp_w_up, mlp_w_down, ple_w_gate, ple_w_proj,
              norm_mix, norm_mlp, norm_ple, final_norm):
    y_prompt = trunk(x_prompt, p_prompt, rel_bias, attn_w_qkv, attn_w_o, attn_sink, conv_w_in, conv_w,
                     conv_w_out, mlp_w_up, mlp_w_down, ple_w_gate, ple_w_proj,
                     norm_mix, norm_mlp, norm_ple, final_norm)
    y_sample = trunk(x_sample, p_sample, rel_bias, attn_w_qkv, attn_w_o, attn_sink, conv_w_in, conv_w,
                     conv_w_out, mlp_w_up, mlp_w_down, ple_w_gate, ple_w_proj,
                     norm_mix, norm_mlp, norm_ple, final_norm)
    return (y_prompt, y_sample)
```

```python
import math
from contextlib import ExitStack

import numpy as np
import concourse.bass as bass
import concourse.mybir as mybir
from concourse.bass_utils import run_bass_kernel_spmd

F32 = mybir.dt.float32
BF16 = mybir.dt.bfloat16
ALU = mybir.AluOpType
AF = mybir.ActivationFunctionType

D = 1024
KC = 8
NH = 16
DEPTH = 4
PLE = 256
DFF = 4096
EPS = 1e-6
OWN = 1024
NEG = -30000.0
REG = 256
EPOCH = 24000
NSLOT = 4
SLOT_BYTES = 8192
LOOKAHEAD = 2


def pieces(lo, hi, step):
    out = []
    a = lo
    while a < hi:
        b = min(hi, (a // step + 1) * step)
        out.append((a, b - a))
        a = b
    return out


def epieces(lo, hi):
    n = hi - lo
    k = (n + 511) // 512
    out = []
    a = lo
    for i in range(k):
        sz = n // k + (1 if i < n % k else 0)
        out.append((a, sz))
        a += sz
    return out


class V:
    __slots__ = ("ap", "regs")

    def __init__(self, ap, regs):
        self.ap = ap
        self.regs = regs


class Buf:
    def __init__(self, base_ap, space_id, off_bytes, shape, es):
        self.space = space_id << 24
        self.off = off_bytes
        self.shape = tuple(shape)
        self.es = es
        self.ap = base_ap
        strides = []
        s = 1
        for d in reversed(self.shape):
            strides.append(s)
            s *= d
        self.strides = tuple(reversed(strides))

    def v(self, *idx, p=None):
        sl = [slice(None) if p is None else slice(p[0], p[1])]
        runs = [0]
        nd = len(self.shape)
        for d, ix in enumerate(idx):
            st = self.strides[d]
            if isinstance(ix, tuple):
                lo, hi = ix
                sl.append(slice(lo, hi))
            else:
                lo, hi = ix, ix + 1
                sl.append(ix if d < nd - 1 else slice(ix, ix + 1))
            if d < nd - 1:
                runs = [r + i * st for r in runs for i in range(lo, hi)]
            else:
                last_lo, last_hi = lo, hi
        regs = set()
        for r in runs:
            b0 = (self.off + (r + last_lo) * self.es) // REG
            b1 = (self.off + (r + last_hi) * self.es - 1) // REG
            for b in range(b0, b1 + 1):
                regs.add(self.space | b)
        return V(self.ap[tuple(sl)], regs)


class Prog:
    ENGS = ("pe", "act", "dve", "pool", "sp")

    def __init__(self):
        self.ops = []
        self.last_w = {}
        self.readers = {}
        self.dma_count = {}

    def add(self, eng, fn, reads, writes, dma_key=None):
        raw = set()
        oth = set()
        lw = self.last_w
        rd = self.readers
        toks_w = set()
        toks_r = set()
        for v in writes:
            for r in v.regs:
                if r >> 24:
                    (toks_w if eng == "pe" else toks_r).add(((r >> 24) << 24) | 0xFFFFFF)
        if eng != "pe":
            for v in reads:
                for r in v.regs:
                    if r >> 24:
                        toks_r.add(((r >> 24) << 24) | 0xFFFFFF)
        if toks_w or toks_r:
            reads = list(reads) + [V(None, toks_r)]
            writes = list(writes) + [V(None, toks_w)]
        for v in reads:
            for r in v.regs:
                w = lw.get(r)
                if w is not None:
                    raw.add(w)
        oid = len(self.ops)
        for v in writes:
            for r in v.regs:
                w = lw.get(r)
                if w is not None:
                    oth.add(w)
                l = rd.get(r)
                if l:
                    oth.update(l)
                lw[r] = oid
                rd[r] = []
        for v in reads:
            for r in v.regs:
                l = rd.get(r)
                if l is None:
                    rd[r] = [oid]
                elif not l or l[-1] != oid:
                    l.append(oid)
        dval = None
        if dma_key is not None:
            dval = self.dma_count.get(dma_key, 0) + 16
            self.dma_count[dma_key] = dval
        self.ops.append([eng, fn, raw, oth, dma_key, dval, False, None])
        return oid

    def mm(self, out, lhsT, rhs, start, stop):
        self.add("pe", lambda e, o=out.ap, l=lhsT.ap, r=rhs.ap, s=start, t=stop: e.matmul(o, l, r, start=s, stop=t),
                 [lhsT, rhs], [out])

    def tr(self, out, in_, ident):
        self.add("pe", lambda e, o=out.ap, i=in_.ap, d=ident.ap: e.transpose(o, i, d), [in_, ident], [out])

    def act(self, out, in_, func, scale=1.0, bias=0.0, accum=None, eng="act"):
        if accum is None:
            self.add(eng, lambda e, o=out.ap, i=in_.ap, f=func, s=scale, b=bias: e.activation(out=o, in_=i, func=f, bias=b, scale=s),
                     [in_], [out])
        else:
            self.add(eng, lambda e, o=out.ap, i=in_.ap, f=func, s=scale, b=bias, a=accum.ap: e.activation(out=o, in_=i, func=f, bias=b, scale=s, accum_out=a),
                     [in_], [out, accum])

    def amul(self, out, in_, c):
        if isinstance(c, V):
            self.add("act", lambda e, o=out.ap, i=in_.ap, m=c.ap: e.mul(o, i, m), [in_, c], [out])
        else:
            self.add("act", lambda e, o=out.ap, i=in_.ap, m=c: e.mul(o, i, m), [in_], [out])

    def tt(self, eng, out, in0, in1, op):
        self.add(eng, lambda e, o=out.ap, a=in0.ap, b=in1.ap, p=op: e.tensor_tensor(o, a, b, p), [in0, in1], [out])

    def stt(self, eng, out, in0, scalar, in1, op0, op1):
        if isinstance(scalar, V):
            self.add(eng, lambda e, o=out.ap, a=in0.ap, s=scalar.ap, b=in1.ap, p=op0, q=op1: e.scalar_tensor_tensor(out=o, in0=a, scalar=s, in1=b, op0=p, op1=q),
                     [in0, in1, scalar], [out])
        else:
            self.add(eng, lambda e, o=out.ap, a=in0.ap, s=scalar, b=in1.ap, p=op0, q=op1: e.scalar_tensor_tensor(out=o, in0=a, scalar=s, in1=b, op0=p, op1=q),
                     [in0, in1], [out])

    def ts(self, eng, out, in0, s1, op0):
        if isinstance(s1, V):
            self.add(eng, lambda e, o=out.ap, a=in0.ap, s=s1.ap, p=op0: e.tensor_scalar(o, a, s, None, p), [in0, s1], [out])
        else:
            self.add(eng, lambda e, o=out.ap, a=in0.ap, s=s1, p=op0: e.tensor_scalar(o, a, s, None, p), [in0], [out])

    def copy(self, eng, out, in_):
        if eng == "act":
            self.add(eng, lambda e, o=out.ap, i=in_.ap: e.copy(o, i), [in_], [out])
        else:
            self.add(eng, lambda e, o=out.ap, i=in_.ap: e.tensor_copy(o, i), [in_], [out])

    def recip(self, out, in_):
        self.add("dve", lambda e, o=out.ap, i=in_.ap: e.reciprocal(o, i), [in_], [out])

    def memset(self, eng, out, val):
        self.add(eng, lambda e, o=out.ap, c=val: e.memset(o, c), [], [out])

    def dma_in(self, eng, out, dram_ap, key):
        self.add(eng, lambda e, o=out.ap, i=dram_ap: e.dma_start(out=o, in_=i), [], [out], dma_key=key)

    def dma_out(self, eng, dram_ap, in_, key):
        self.add(eng, lambda e, o=dram_ap, i=in_.ap: e.dma_start(out=o, in_=i), [in_], [], dma_key=key)

    def emit(self, nc, es):
        ops = self.ops
        eng_of = [o[0] for o in ops]
        is_dma = [o[4] is not None for o in ops]
        need = []
        for oid, o in enumerate(ops):
            eng = o[0]
            cdeps = {}
            ddeps = {}
            for src, is_raw in ((o[2], True), (o[3], False)):
                for d in src:
                    if is_dma[d]:
                        k = ops[d][4]
                        if ops[d][5] > ddeps.get(k, 0):
                            ddeps[k] = ops[d][5]
                    else:
                        de = eng_of[d]
                        if de == eng and not is_dma[oid]:
                            if eng == "pe" or not is_raw:
                                continue
                        if d > cdeps.get(de, -1):
                            cdeps[de] = d
            for d in cdeps.values():
                ops[d][6] = True
            need.append((cdeps, ddeps))
        cnt = {e: 0 for e in self.ENGS}
        for o in ops:
            if o[6]:
                c = cnt[o[0]]
                o[7] = (c // EPOCH, c % EPOCH + 1)
                cnt[o[0]] = c + 1
        sems = {}
        for e in self.ENGS:
            for ep in range(cnt[e] // EPOCH + 1):
                sems[(e, ep)] = es.enter_context(nc.semaphore(f"s_{e}_{ep}"))
        dsem = {}
        for k in self.dma_count:
            dsem[k] = es.enter_context(nc.semaphore(f"d_{k}"))
        block = es.enter_context(nc.Block())
        per_eng = {e: [] for e in self.ENGS}
        for oid, o in enumerate(ops):
            per_eng[o[0]].append(oid)

        def run_engine(eng_name, e):
            waited = {}
            for oid in per_eng[eng_name]:
                o = ops[oid]
                cdeps, ddeps = need[oid]
                for de, d in cdeps.items():
                    sv = ops[d][7]
                    if waited.get(de, (-1, 0)) >= sv:
                        continue
                    e.wait_ge(sems[(de, sv[0])], sv[1])
                    waited[de] = sv
                for k, val in ddeps.items():
                    kk = ("dma", k, 0)
                    if waited.get(kk, 0) >= val:
                        continue
                    e.wait_ge(dsem[k], val)
                    waited[kk] = val
                ins = o[1](e)
                if o[4] is not None:
                    ins.then_inc(dsem[o[4]], 16)
                elif o[6]:
                    ins.then_inc(sems[(eng_name, o[7][0])], 1)
            if eng_name == "sp":
                for k, total in self.dma_count.items():
                    e.wait_ge(dsem[k], total)

        @block.tensor
        def _(e):
            run_engine("pe", e)

        @block.scalar
        def _(e):
            run_engine("act", e)

        @block.vector
        def _(e):
            run_engine("dve", e)

        @block.gpsimd
        def _(e):
            run_engine("pool", e)

        @block.sync
        def _(e):
            run_engine("sp", e)


def make_tiles(seq_lens):
    tiles = []
    row0 = 0
    for L in seq_lens:
        n = L // OWN
        for i in range(n):
            left = i > 0
            right = i < n - 1
            if left and right:
                hl, hr = 320, 320
            elif left:
                hl, hr = 384, 0
            elif right:
                hl, hr = 0, 384
            else:
                hl, hr = 0, 0
            T = hl + OWN + hr
            g0 = row0 + i * OWN - hl
            own = (hl, hl + OWN)
            r3 = own
            r2 = (own[0] - (2 if left else 0), own[1] + (2 if right else 0))
            r1 = (own[0] - (130 if left else 0), own[1] + (130 if right else 0))
            r0 = (own[0] - (132 if left else 0), own[1] + (132 if right else 0))
            tiles.append(dict(T=T, g0=g0, own=own, R=[r0, r1, r2, r3]))
        row0 += L
    return tiles


def build_program(seq_lens, depth=DEPTH):
    ntok = sum(seq_lens)
    nc = bass.Bass("TRN2", target_bir_lowering=False)
    dt = nc.dram_tensor
    xtok = dt("xtok", [ntok, D], F32, kind="ExternalInput").ap()
    ptok = dt("ptok", [DEPTH, ntok, PLE], F32, kind="ExternalInput").ap()
    w_qkv = dt("w_qkv", [2, D, 1536], F32, kind="ExternalInput").ap()
    w_o = dt("w_o", [2, D, D], F32, kind="ExternalInput").ap()
    w_cin = dt("w_cin", [2, D, 3 * D], F32, kind="ExternalInput").ap()
    w_cout = dt("w_cout", [2, D, D], F32, kind="ExternalInput").ap()
    w_up = dt("w_up", [DEPTH, D, DFF], F32, kind="ExternalInput").ap()
    w_down = dt("w_down", [DEPTH, DFF, D], F32, kind="ExternalInput").ap()
    w_gate = dt("w_gate", [DEPTH, D, D], F32, kind="ExternalInput").ap()
    w_proj = dt("w_proj", [DEPTH, PLE, D], F32, kind="ExternalInput").ap()
    c_ident = dt("c_ident", [128, 128], F32, kind="ExternalInput").ap()
    c_bt = dt("c_bt", [128, NH * 384], F32, kind="ExternalInput").ap()
    c_g = dt("c_g", [128, 13 * 8], F32, kind="ExternalInput").ap()
    c_cw = dt("c_cw", [128, 2 * 3 * 8], F32, kind="ExternalInput").ap()
    c_sink = dt("c_sink", [1, 2 * NH], F32, kind="ExternalInput").ap()
    c_fn = dt("c_fn", [1, D], F32, kind="ExternalInput").ap()
    ytok = dt("ytok", [ntok, D], F32, kind="ExternalOutput").ap()

    tiles = make_tiles(seq_lens)
    TM = max(t["T"] for t in tiles)
    NBM = TM // 128

    es = ExitStack()
    with es:
        lay = {}
        cur = [0]

        def alloc(name, nbytes):
            lay[name] = cur[0]
            cur[0] += (nbytes + 63) // 64 * 64

        alloc("x", KC * TM * 4)
        alloc("H", KC * TM * 2)
        alloc("W", NSLOT * SLOT_BYTES)
        alloc("BT", NH * 384 * 4)
        alloc("VA", NBM * 192 * 2)
        alloc("ident", 512)
        alloc("ones", 256)
        alloc("G", 13 * 8 * 4)
        alloc("CW", 48 * 4)
        alloc("ES", 32 * 4)
        alloc("GB", D * 4)
        alloc("IO", 2 * D * 4)
        alloc("PT", 2 * TM * 2)
        alloc("SQ", KC * 512 * 2)
        alloc("RS", 2 * 512 * 4)
        alloc("SS", 64)
        arena0 = cur[0]
        alloc("KTa", TM * 2)
        alloc("KTb", TM * 2)
        alloc("QT", 2 * TM * 2)
        alloc("OT", 2 * TM * 2)
        alloc("EB", 6 * 384 * 2)
        alloc("RC", 2 * 128 * 4)
        arena_end = cur[0]
        cur[0] = arena0
        alloc("Z", (TM + 2) * 4)
        alloc("Bf", TM * 4)
        alloc("Y", 4 * TM * 2)
        alloc("Ct", 2 * 512 * 4)
        alloc("AC", 2 * 512 * 4)
        arena_end = max(arena_end, cur[0])
        cur[0] = arena0
        alloc("A", 2 * 4 * 512 * 2)
        alloc("SQT", 2 * 512 * 4)
        arena_end = max(arena_end, cur[0])
        cur[0] = arena0
        alloc("Gt", 2 * 512 * 4)
        alloc("TMP", 2 * 512 * 4)
        arena_end = max(arena_end, cur[0])
        lay["PS"] = arena0 + 12288
        arena_end = max(arena_end, arena0 + 12288 + 12 * PLE * 4)
        lay["XIN"] = arena0 + 8192
        arena_end = max(arena_end, arena0 + 8192 + 4 * D * 4)
        total = arena_end
        space = es.enter_context(nc.sbuf_tensor("space", [128, total // 2], BF16))
        S = space[:, :]

        def mk(name, shape, dtype):
            esz = 4 if dtype is F32 else 2
            n = int(np.prod(shape))
            o = lay[name]
            ap = S[:, o // 2: o // 2 + n * esz // 2]
            if dtype is F32:
                ap = ap.bitcast(F32)
            if len(shape) == 2:
                ap = ap.rearrange("p (a b) -> p a b", b=shape[1])
            elif len(shape) == 3:
                ap = ap.rearrange("p (a b c) -> p a b c", b=shape[1], c=shape[2])
            return Buf(ap, 0, o, shape, esz)

        def mk_at(off, shape, dtype):
            esz = 4 if dtype is F32 else 2
            n = int(np.prod(shape))
            ap = S[:, off // 2: off // 2 + n * esz // 2]
            if dtype is F32:
                ap = ap.bitcast(F32)
            if len(shape) == 2:
                ap = ap.rearrange("p (a b) -> p a b", b=shape[1])
            return Buf(ap, 0, off, shape, esz)

        X = mk("x", (KC, TM), F32)
        H = mk("H", (KC, TM), BF16)
        BT = mk("BT", (NH, 3, 128), F32)
        BTflat = mk("BT", (NH * 384,), F32)
        VA = mk("VA", (NBM, 192), BF16)
        VAflat = mk("VA", (NBM * 192,), BF16)
        IDN = mk("ident", (128,), F32)
        ONES = mk("ones", (128,), BF16)
        G = mk("G", (13, 8), F32)
        Gflat = mk("G", (104,), F32)
        CW = mk("CW", (2, 3, 8), F32)
        CWflat = mk("CW", (48,), F32)
        ESb = mk("ES", (2, NH), F32)
        ESflat = mk("ES", (32,), F32)
        GB = mk("GB", (D,), F32)
        IO = mk("IO", (2, D), F32)
        XIN = mk("XIN", (4, D), F32)
        PSt = mk("PS", (12, PLE), F32)
        PT = mk("PT", (2, TM), BF16)
        SQ = mk("SQ", (KC, 512), BF16)
        RS = mk("RS", (2, 512), F32)
        SS = mk("SS", (16,), F32)
        KTa = mk("KTa", (TM,), BF16)
        KTb = mk("KTb", (TM,), BF16)
        QT = mk("QT", (2, TM), BF16)
        OT = mk("OT", (2, TM), BF16)
        EB = mk("EB", (6, 3, 128), BF16)
        RC = mk("RC", (2, 128), F32)
        Z = mk("Z", (TM + 2,), F32)
        Bf = mk("Bf", (TM,), F32)
        Y = mk("Y", (4, TM), BF16)
        Ct = mk("Ct", (2, 512), F32)
        AC = mk("AC", (2, 512), F32)
        Abuf = mk("A", (2, 4, 512), BF16)
        SQT = mk("SQT", (2, 512), F32)
        Gt = mk("Gt", (2, 512), F32)
        TMP = mk("TMP", (2, 512), F32)

        banks = []
        for b in range(8):
            t = es.enter_context(nc.psum_tensor(f"psb{b}", [128, 512], F32))
            banks.append(Buf(t[:, :], 1 + b, 0, (512,), 4))
        banks3 = [Buf(bk.ap.rearrange("p (a b) -> p a b", b=128), 1 + i, 0, (4, 128), 4) for i, bk in enumerate(banks)]
        mmrot = [0]

        def mmbank():
            b = mmrot[0] % 4
            mmrot[0] += 1
            return b

        P = Prog()

        P.dma_in("sp", IDN.v((0, 128)), c_ident, "c0")
        P.dma_in("sp", BTflat.v((0, NH * 384)), c_bt, "c1")
        P.dma_in("sp", Gflat.v((0, 104)), c_g, "c2")
        P.dma_in("sp", CWflat.v((0, 48)), c_cw, "c3")
        P.dma_in("sp", ESflat.v((0, 32)), c_sink.partition_broadcast(128), "c4")
        P.dma_in("sp", GB.v((0, D)), c_fn.partition_broadcast(128), "c5")
        P.memset("pool", ONES.v((0, 128)), 1.0)
        P.memset("pool", VAflat.v((0, NBM * 192)), 1.0)
        P.act(ESflat.v((0, 32)), ESflat.v((0, 32)), AF.Exp)

        stages = []
        wslot_ctr = [0]

        def slot_buf(si, shape, elem_off=0):
            return mk_at(lay["W"] + si * SLOT_BYTES + elem_off * 2, shape, BF16)

        def wdma(si, part, view, dram_ap):
            P.dma_in("pool", view, dram_ap, f"w{si}_{part}")

        def norm_stats(a, n, k):
            P.act(SQ.v((0, KC), (0, n)), X.v((0, KC), (a, a + n)), AF.Square)
            nb = banks[7]
            for c in range(KC):
                P.mm(nb.v((0, n)), ONES.v((0, 128)), SQ.v(c, (0, n)), c == 0, c == KC - 1)
            P.act(RS.v(k, (0, n)), nb.v((0, n)), AF.Ln, scale=1.0 / D, bias=EPS)
            P.act(RS.v(k, (0, n)), RS.v(k, (0, n)), AF.Exp, scale=-0.5)

        nrm_k = [0]

        def norm_S(a, n):
            P.act(SQ.v((0, KC), (0, n)), X.v((0, KC), (a, a + n)), AF.Square)

        def norm_M(gidx, a, n):
            k = nrm_k[0] % 2
            nrm_k[0] += 1
            nb = banks[7]
            for c in range(KC):
                P.mm(nb.v((0, n)), ONES.v((0, 128)), SQ.v(c, (0, n)), c == 0, c == KC - 1)
            P.act(RS.v(k, (0, n)), nb.v((0, n)), AF.Ln, scale=1.0 / D, bias=EPS)
            P.act(RS.v(k, (0, n)), RS.v(k, (0, n)), AF.Exp, scale=-0.5)
            for c in range(KC):
                P.stt("dve", H.v(c, (a, a + n)), X.v(c, (a, a + n)), G.v(gidx, c), RS.v(k, (0, n)), ALU.mult, ALU.mult)

        def norm_to_H(gidx, lo, hi):
            for (a, n) in epieces(lo, hi):
                norm_S(a, n)
                norm_M(gidx, a, n)

        class FusedNorm:
            def __init__(self, gidx):
                self.gidx = gidx
                self.pend = None

            def after_piece(self, a, n):
                if self.pend is not None:
                    norm_M(self.gidx, *self.pend)
                norm_S(a, n)
                self.pend = (a, n)

            def finish(self):
                if self.pend is not None:
                    norm_M(self.gidx, *self.pend)
                    self.pend = None

        def load_tile(t):
            T, g0 = t["T"], t["g0"]
            for b in range(T // 128):
                k = b % 4
                if not (b < 4 and t.get("pref")):
                    P.dma_in("sp", XIN.v(k, (0, D)), xtok[g0 + b * 128: g0 + (b + 1) * 128, :], f"xin{k}")
                for half in range(2):
                    bk = mmbank()
                    for cc in range(4):
                        c = half * 4 + cc
                        P.tr(banks[bk].v((cc * 128, cc * 128 + 128)), XIN.v(k, (c * 128, c * 128 + 128)), IDN.v((0, 128)))
                    P.copy("act" if half == 0 else "dve", X.v((half * 4, half * 4 + 4), (b * 128, b * 128 + 128)), banks3[bk].v((0, 4), (0, 128)))

        def attn_layer(t, l):
            j = l // 2
            T = t["T"]
            qlo, qhi = t["R"][l]
            kb0 = max(0, qlo - 128) // 128
            kb1 = (min(T, qhi + 128) + 127) // 128
            klo, khi = kb0 * 128, kb1 * 128

            for g in range(4):
                def load_ag(si, g=g):
                    sb = slot_buf(si, (KC, 448))
                    wq = w_qkv[j]
                    wdma(si, 0, sb.v((0, KC), (0, 256)), wq[:, g * 256:(g + 1) * 256].rearrange("(k p) n -> p k n", p=128))
                    wdma(si, 1, sb.v((0, KC), (256, 320)), wq[:, 1024 + g * 64:1024 + (g + 1) * 64].rearrange("(k p) n -> p k n", p=128))
                    wdma(si, 2, sb.v((0, KC), (320, 384)), wq[:, 1024 + g * 64:1024 + (g + 1) * 64].rearrange("(k p) n -> p k n", p=128))
                    wdma(si, 3, sb.v((0, KC), (384, 448)), wq[:, 1280 + g * 64:1280 + (g + 1) * 64].rearrange("(k p) n -> p k n", p=128))

                def comp_ag(si, g=g):
                    sb = slot_buf(si, (KC, 448))
                    if g == 0:
                        norm_to_H(l, klo, khi)
                        P.memset("pool", KTa.v((klo, khi), p=(64, 128)), 0.0)
                        P.memset("pool", KTb.v((klo, khi), p=(0, 64)), 0.0)
                    for (a, n) in pieces(klo, khi, 512):
                        bk = mmbank()
                        for c in range(KC):
                            P.mm(banks[bk].v((0, n)), sb.v(c, (256, 384)), H.v(c, (a, a + n)), c == 0, c == KC - 1)
                        P.copy("act", KTa.v((a, a + n), p=(0, 64)), banks[bk].v((0, n), p=(0, 64)))
                        P.copy("act", KTb.v((a, a + n), p=(64, 128)), banks[bk].v((0, n), p=(64, 128)))
                    kb = kb0
                    while kb < kb1:
                        nb_ = min(4, kb1 - kb)
                        bk = mmbank()
                        for i in range(nb_):
                            for c in range(KC):
                                P.mm(banks[bk].v((i * 64, i * 64 + 64)), H.v(c, ((kb + i) * 128, (kb + i) * 128 + 128)), sb.v(c, (384, 448)), c == 0, c == KC - 1)
                        b3 = Buf(banks[bk].ap.rearrange("p (a b) -> p a b", b=64), 1 + bk, 0, (8, 64), 4)
                        P.copy("dve", VA.v((kb, kb + nb_), (64, 128)), b3.v((0, nb_), (0, 64)))
                        kb += nb_
                    for (a, n) in epieces(qlo, qhi):
                        for cl in range(2):
                            bk = mmbank()
                            for c in range(KC):
                                P.mm(banks[bk].v((0, n)), sb.v(c, (cl * 128, cl * 128 + 128)), H.v(c, (a, a + n)), c == 0, c == KC - 1)
                            P.amul(QT.v(cl, (a, a + n)), banks[bk].v((0, n)), 0.125)
                    units = [(qs, nq, cl) for (qs, nq) in pieces(qlo, qhi, 128) for cl in range(2)]

                    def geom(qs):
                        jb = qs // 128
                        ois = [oi for oi in range(3) if kb0 <= jb + oi - 1 < kb1]
                        return jb, qs % 128, ois

                    def emit_A(ui):
                        qs, nq, cl = units[ui]
                        jb, qa, ois = geom(qs)
                        for p in range(2):
                            sbk = (ui % 3) * 2 + p
                            KT = KTa if p == 0 else KTb
                            for oi in ois:
                                kbx = jb + oi - 1
                                P.mm(banks3[sbk].v(oi, (qa, qa + nq)), KT.v((kbx * 128, kbx * 128 + 128)), QT.v(cl, (qs, qs + nq)), True, True)

                    def emit_B(ui):
                        qs, nq, cl = units[ui]
                        jb, qa, ois = geom(qs)
                        o0, o1 = ois[0], ois[-1] + 1
                        for p in range(2):
                            h = g * 4 + cl * 2 + p
                            sbk = (ui % 3) * 2 + p
                            k2 = (ui * 2 + p) % 6
                            sv = banks3[sbk].v((o0, o1), (qa, qa + nq))
                            P.tt("dve", sv, sv, BT.v(h, (o0, o1), (qa, qa + nq)), ALU.add)
                            P.act(EB.v(k2, (o0, o1), (qa, qa + nq)), sv, AF.Exp)

                    def emit_C(ui):
                        qs, nq, cl = units[ui]
                        jb, qa, ois = geom(qs)
                        ob = banks3[6 + ui % 2]
                        for p in range(2):
                            k2 = (ui * 2 + p) % 6
                            for oi in ois:
                                kbx = jb + oi - 1
                                va = VA.v(kbx, (64, 192)) if p == 0 else VA.v(kbx, (0, 128))
                                P.mm(ob.v(p, (0, nq)), va, EB.v(k2, oi, (qa, qa + nq)), oi == ois[0], oi == ois[-1])

                    def emit_D(ui):
                        qs, nq, cl = units[ui]
                        obk = 6 + ui % 2
                        ob = banks3[obk]
                        b6 = banks[obk].ap.rearrange("p (a b) -> p a b", b=128)
                        rk = ui % 2
                        for p in range(2):
                            data = (0, 64) if p == 0 else (64, 128)
                            den = (64, 128) if p == 0 else (0, 64)
                            h = g * 4 + cl * 2 + p
                            oregs = ob.v(p, (0, nq)).regs
                            P.ts("dve", RC.v(rk, (0, nq), p=data), V(b6[den[0]:den[1], p, 0:nq], oregs), ESb.v(j, h, p=data), ALU.add)
                        P.act(RC.v(rk, (0, nq)), RC.v(rk, (0, nq)), AF.Ln)
                        P.act(RC.v(rk, (0, nq)), RC.v(rk, (0, nq)), AF.Exp, scale=-1.0)

                    def emit_D2(ui):
                        qs, nq, cl = units[ui]
                        obk = 6 + ui % 2
                        ob = banks3[obk]
                        b6 = banks[obk].ap.rearrange("p (a b) -> p a b", b=128)
                        rk = ui % 2
                        for p in range(2):
                            data = (0, 64) if p == 0 else (64, 128)
                            oregs = ob.v(p, (0, nq)).regs
                            P.tt("dve", OT.v(cl, (qs, qs + nq), p=data), V(b6[data[0]:data[1], p, 0:nq], oregs), RC.v(rk, (0, nq), p=data), ALU.mult)

                    nu = len(units)
                    for u0 in range(min(3, nu)):
                        emit_A(u0)
                    for u0 in range(min(2, nu)):
                        emit_B(u0)
                    for ui in range(nu):
                        if ui + 3 < nu:
                            emit_A(ui + 3)
                        emit_C(ui)
                        emit_D(ui)
                        if ui + 2 < nu:
                            emit_B(ui + 2)
                        emit_D2(ui)

                stages.append((load_ag, comp_ag))

                def load_ao(si, g=g):
                    sb = slot_buf(si, (2, D))
                    wdma(si, 0, sb.v((0, 2), (0, D)), w_o[j][g * 256:(g + 1) * 256, :].rearrange("(k p) n -> p k n", p=128))

                def comp_ao(si, g=g):
                    sb = slot_buf(si, (2, D))
                    fn = FusedNorm(4 + l) if g == 3 else None
                    for (a, n) in epieces(qlo, qhi):
                        for m in range(KC):
                            bk = mmbank()
                            for c in range(2):
                                P.mm(banks[bk].v((0, n)), sb.v(c, (m * 128, m * 128 + 128)), OT.v(c, (a, a + n)), c == 0, c == 1)
                            P.tt("dve", X.v(m, (a, a + n)), X.v(m, (a, a + n)), banks[bk].v((0, n)), ALU.add)
                        if fn:
                            fn.after_piece(a, n)
                    if fn:
                        fn.finish()

                stages.append((load_ao, comp_ao))

        def conv_layer(t, l):
            j = l // 2
            T = t["T"]
            lo, hi = t["R"][l]
            ilo, ihi = max(0, lo - 2), min(T, hi + 2)
            for fc in range(KC):
                def load_cv(si, fc=fc):
                    sb = slot_buf(si, (KC, 384))
                    wi = w_cin[j]
                    for part in range(3):
                        wdma(si, part, sb.v((0, KC), (part * 128, part * 128 + 128)),
                             wi[:, part * D + fc * 128: part * D + (fc + 1) * 128].rearrange("(k p) n -> p k n", p=128))

                def comp_cv(si, fc=fc):
                    sb = slot_buf(si, (KC, 384))
                    if fc == 0:
                        norm_to_H(l, ilo, ihi)
                    P.memset("pool", Z.v((ilo, ilo + 1)), 0.0)
                    P.memset("pool", Z.v((ihi + 1, ihi + 2)), 0.0)
                    ck = 0
                    for (a, n) in epieces(ilo, ihi):
                        bks = []
                        for part in range(3):
                            bk = mmbank()
                            bks.append(bk)
                            for c in range(KC):
                                P.mm(banks[bk].v((0, n)), sb.v(c, (part * 128, part * 128 + 128)), H.v(c, (a, a + n)), c == 0, c == KC - 1)
                        k = ck % 2
                        ck += 1
                        P.copy("act", Bf.v((a, a + n)), banks[bks[0]].v((0, n)))
                        P.copy("act", Ct.v(k, (0, n)), banks[bks[1]].v((0, n)))
                        P.tt("dve", Z.v((a + 1, a + 1 + n)), banks[bks[2]].v((0, n)), Ct.v(k, (0, n)), ALU.mult)
                    for (a, n) in epieces(lo, hi):
                        k = ck % 2
                        ck += 1
                        acc = AC.v(k, (0, n))
                        P.amul(acc, Z.v((a, a + n)), CW.v(j, 0, fc))
                        P.stt("dve", acc, Z.v((a + 1, a + 1 + n)), CW.v(j, 1, fc), acc, ALU.mult, ALU.add)
                        P.stt("dve", acc, Z.v((a + 2, a + 2 + n)), CW.v(j, 2, fc), acc, ALU.mult, ALU.add)
                        P.tt("dve", Y.v(fc % 4, (a, a + n)), acc, Bf.v((a, a + n)), ALU.mult)

                stages.append((load_cv, comp_cv))
                if fc % 4 == 3:
                    half = fc // 4

                    def load_co(si, half=half):
                        so = slot_buf(si, (4, D))
                        wdma(si, 0, so.v((0, 4), (0, D)), w_cout[j][half * 512:(half + 1) * 512, :].rearrange("(k p) n -> p k n", p=128))

                    def comp_co(si, half=half):
                        so = slot_buf(si, (4, D))
                        fn = FusedNorm(4 + l) if half == 1 else None
                        for (a, n) in epieces(lo, hi):
                            for m in range(KC):
                                bk = mmbank()
                                for c in range(4):
                                    P.mm(banks[bk].v((0, n)), so.v(c, (m * 128, m * 128 + 128)), Y.v(c, (a, a + n)), c == 0, c == 3)
                                P.tt("dve", X.v(m, (a, a + n)), X.v(m, (a, a + n)), banks[bk].v((0, n)), ALU.add)
                            if fn:
                                fn.after_piece(a, n)
                        if fn:
                            fn.finish()

                    stages.append((load_co, comp_co))

        def mlp_layer(t, l):
            lo, hi = t["R"][l]
            for f in range(8):
                def load_u(si, f=f):
                    sb = slot_buf(si, (KC, 512))
                    wdma(si, 0, sb.v((0, KC), (0, 512)), w_up[l][:, f * 512:(f + 1) * 512].rearrange("(k p) n -> p k n", p=128))

                hold = {}

                def comp_u(si, f=f, hold=hold):
                    hold["u"] = si
                    if f == 4:
                        b0, b1 = lo // 128, (hi + 127) // 128
                        for b in range(b0, b1):
                            P.dma_in("sp", PSt.v(b - b0, (0, PLE)), ptok[l][t["g0"] + b * 128: t["g0"] + (b + 1) * 128, :], f"ps{b - b0}")

                stages.append((load_u, comp_u))

                def load_d(si, f=f):
                    sb = slot_buf(si, (4, D))
                    wdma(si, 0, sb.v((0, 4), (0, D)), w_down[l][f * 512:(f + 1) * 512, :].rearrange("(k p) n -> p k n", p=128))

                def comp_d(si, f=f, hold=hold):
                    su = slot_buf(hold["u"], (KC, 512))
                    sd = slot_buf(si, (4, D))
                    ak = 0
                    fn = FusedNorm(8 + l) if f == 7 else None
                    for (a, n) in epieces(lo, hi):
                        ka = ak % 2
                        ak += 1
                        for m in range(4):
                            bk = mmbank()
                            for c in range(KC):
                                P.mm(banks[bk].v((0, n)), su.v(c, (m * 128, m * 128 + 128)), H.v(c, (a, a + n)), c == 0, c == KC - 1)
                            k = m % 2
                            P.act(SQT.v(k, (0, n)), banks[bk].v((0, n)), AF.Square)
                            P.stt("dve", Abuf.v(ka, m, (0, n)), banks[bk].v((0, n)), 0.0, SQT.v(k, (0, n)), ALU.is_gt, ALU.mult)
                        for m in range(KC):
                            bk = mmbank()
                            for c in range(4):
                                P.mm(banks[bk].v((0, n)), sd.v(c, (m * 128, m * 128 + 128)), Abuf.v(ka, c, (0, n)), c == 0, c == 3)
                            P.tt("dve", X.v(m, (a, a + n)), X.v(m, (a, a + n)), banks[bk].v((0, n)), ALU.add)
                        if fn:
                            fn.after_piece(a, n)
                    if fn:
                        fn.finish()

                stages.append((load_d, comp_d))

        def ple_layer(t, l):
            lo, hi = t["R"][l]
            g0 = t["g0"]
            for hh in range(4):
                def load_pg(si, hh=hh):
                    sg = slot_buf(si, (KC, 256))
                    sp_ = slot_buf(si, (2, 256), elem_off=KC * 256)
                    wdma(si, 0, sg.v((0, KC), (0, 256)), w_gate[l][:, hh * 256:(hh + 1) * 256].rearrange("(k p) n -> p k n", p=128))
                    wdma(si, 1, sp_.v((0, 2), (0, 256)), w_proj[l][:, hh * 256:(hh + 1) * 256].rearrange("(k p) n -> p k n", p=128))

                def comp_pg(si, hh=hh):
                    sg = slot_buf(si, (KC, 256))
                    sp_ = slot_buf(si, (2, 256), elem_off=KC * 256)
                    if hh == 0:
                        b0, b1 = lo // 128, (hi + 127) // 128
                        for b in range(b0, b1):
                            k = b - b0
                            bk = mmbank()
                            for c in range(2):
                                P.tr(banks[bk].v((c * 128, c * 128 + 128)), PSt.v(k, (c * 128, c * 128 + 128)), IDN.v((0, 128)))
                            P.copy("act", PT.v((0, 2), (b * 128, b * 128 + 128)), banks3[bk].v((0, 2), (0, 128)))
                    kk = 0
                    for (a, n) in epieces(lo, hi):
                        for m in range(2):
                            mg = hh * 2 + m
                            bg = mmbank()
                            for c in range(KC):
                                P.mm(banks[bg].v((0, n)), sg.v(c, (m * 128, m * 128 + 128)), H.v(c, (a, a + n)), c == 0, c == KC - 1)
                            bp = mmbank()
                            for c in range(2):
                                P.mm(banks[bp].v((0, n)), sp_.v(c, (m * 128, m * 128 + 128)), PT.v(c, (a, a + n)), c == 0, c == 1)
                            k = kk % 2
                            kk += 1
                            P.act(Gt.v(k, (0, n)), banks[bg].v((0, n)), AF.Sigmoid)
                            P.tt("dve", TMP.v(k, (0, n)), banks[bp].v((0, n)), Gt.v(k, (0, n)), ALU.mult)
                            P.tt("dve", X.v(mg, (a, a + n)), X.v(mg, (a, a + n)), TMP.v(k, (0, n)), ALU.add)

                stages.append((load_pg, comp_pg))

        def final_tile(t, nxt=None):
            lo, hi = t["own"]
            g0 = t["g0"]
            if nxt is not None:
                for b in range(4):
                    P.dma_in("sp", XIN.v(b, (0, D)), xtok[nxt["g0"] + b * 128: nxt["g0"] + (b + 1) * 128, :], f"xin{b}")
                nxt["pref"] = True
            for bi in range((hi - lo) // 128):
                k = bi % 2
                ta = lo + bi * 128
                bks = []
                for half in range(2):
                    bk = mmbank()
                    bks.append(bk)
                    for cc in range(4):
                        c = half * 4 + cc
                        P.tr(banks[bk].v((cc * 128, cc * 128 + 128)), X.v(c, (ta, ta + 128)), IDN.v((0, 128)))
                P.memset("pool", SS.v((k * 4, k * 4 + 2)), 0.0)
                for half in range(2):
                    P.act(IO.v(k, (half * 512, half * 512 + 512)), banks[bks[half]].v((0, 512)), AF.Square, accum=SS.v((k * 4 + half, k * 4 + half + 1)))
                P.tt("dve", SS.v((k * 4 + 2, k * 4 + 3)), SS.v((k * 4, k * 4 + 1)), SS.v((k * 4 + 1, k * 4 + 2)), ALU.add)
                P.act(SS.v((k * 4 + 3, k * 4 + 4)), SS.v((k * 4 + 2, k * 4 + 3)), AF.Sqrt, scale=1.0 / D, bias=EPS)
                P.recip(SS.v((k * 4 + 2, k * 4 + 3)), SS.v((k * 4 + 3, k * 4 + 4)))
                for half in range(2):
                    P.stt("dve", IO.v(k, (half * 512, half * 512 + 512)), banks[bks[half]].v((0, 512)), SS.v((k * 4 + 2, k * 4 + 3)),
                          GB.v((half * 512, half * 512 + 512)), ALU.mult, ALU.mult)
                P.dma_out("sp", ytok[g0 + ta: g0 + ta + 128, :], IO.v(k, (0, D)), f"io{k}")

        for ti, t in enumerate(tiles):
            stages.append((None, lambda si, t=t: load_tile(t)))
            for l in range(depth):
                if l % 2 == 0:
                    attn_layer(t, l)
                else:
                    conv_layer(t, l)
                mlp_layer(t, l)
                ple_layer(t, l)
            nxt = tiles[ti + 1] if ti + 1 < len(tiles) else None
            stages.append((None, lambda si, t=t, nxt=nxt: final_tile(t, nxt)))

        wstages = [i for i, s in enumerate(stages) if s[0] is not None]
        slot_of = {si: (n % NSLOT) for n, si in enumerate(wstages)}
        issued = 0

        def issue_upto(n):
            nonlocal issued
            while issued < min(n, len(wstages)):
                si = wstages[issued]
                stages[si][0](slot_of[si])
                issued += 1

        nw = 0
        for i, (ld, cp) in enumerate(stages):
            if ld is not None:
                issue_upto(nw + 1 + LOOKAHEAD)
                nw += 1
            else:
                issue_upto(nw + LOOKAHEAD)
            cp(slot_of.get(i, -1))

        P.emit(nc, es)
    return nc


_BUCKETS = None


def _bucket_table():
    global _BUCKETS
    if _BUCKETS is None:
        import jax
        import jax.numpy as jnp
        with jax.default_device(jax.devices("cpu")[0]):
            rel = jnp.arange(-255, 256)
            half = 16
            max_exact = 8
            ret = jnp.where(rel > 0, half, 0)
            n = jnp.abs(rel)
            nf = jnp.maximum(n, 1).astype(jnp.float32)
            large = max_exact + (jnp.log(nf / max_exact) / math.log(128 / max_exact) * (half - max_exact)).astype(jnp.int32)
            large = jnp.minimum(large, half - 1)
            _BUCKETS = np.asarray(ret + jnp.where(n < max_exact, n, large))
    return _BUCKETS


def _consts(rel_bias, attn_sink, conv_w, norm_mix, norm_mlp, norm_ple, final_norm):
    bucket = _bucket_table()
    k = np.arange(128)[:, None, None]
    oi = np.arange(3)[None, :, None]
    q = np.arange(128)[None, None, :]
    rel = (oi - 1) * 128 + k - q
    idx = bucket[rel + 255]
    bt = np.asarray(rel_bias, np.float32)[idx]
    bt = np.where((np.abs(rel) <= 128)[..., None], bt, np.float32(NEG))
    bt = np.ascontiguousarray(np.transpose(bt, (0, 3, 1, 2))).reshape(128, NH * 384).astype(np.float32)
    gains = np.concatenate([norm_mix, norm_mlp, norm_ple, np.asarray(final_norm)[None]], axis=0)
    g = np.ascontiguousarray(np.transpose(np.asarray(gains, np.float32).reshape(13, 8, 128), (2, 0, 1))).reshape(128, 104)
    cw = np.ascontiguousarray(np.transpose(np.asarray(conv_w, np.float32).reshape(2, 3, 8, 128), (3, 0, 1, 2))).reshape(128, 48)
    return dict(c_ident=np.eye(128, dtype=np.float32), c_bt=bt, c_g=g, c_cw=cw,
                c_sink=np.asarray(attn_sink, np.float32).reshape(1, 32),
                c_fn=np.asarray(final_norm, np.float32).reshape(1, D))


_NC_CACHE = {}


def kernel(x_prompt, x_sample, p_prompt, p_sample, rel_bias, attn_w_qkv, attn_w_o, attn_sink,
           conv_w_in, conv_w, conv_w_out, mlp_w_up, mlp_w_down, ple_w_gate, ple_w_proj,
           norm_mix, norm_mlp, norm_ple, final_norm):
    n = 8
    x_prompt = np.asarray(x_prompt, np.float32)
    x_sample = np.asarray(x_sample, np.float32)
    p_prompt = np.asarray(p_prompt, np.float32)
    p_sample = np.asarray(p_sample, np.float32)
    SP, SS_ = x_prompt.shape[1], x_sample.shape[1]
    seq_lens = (SP, SP, SS_)
    key = seq_lens
    if key not in _NC_CACHE:
        _NC_CACHE[key] = build_program(list(seq_lens))
    nc = _NC_CACHE[key]
    shared = dict(
        w_qkv=np.ascontiguousarray(attn_w_qkv, np.float32), w_o=np.ascontiguousarray(attn_w_o, np.float32),
        w_cin=np.ascontiguousarray(conv_w_in, np.float32), w_cout=np.ascontiguousarray(conv_w_out, np.float32),
        w_up=np.ascontiguousarray(mlp_w_up, np.float32), w_down=np.ascontiguousarray(mlp_w_down, np.float32),
        w_gate=np.ascontiguousarray(ple_w_gate, np.float32), w_proj=np.ascontiguousarray(ple_w_proj, np.float32),
    )
    shared.update(_consts(rel_bias, attn_sink, conv_w, norm_mix, norm_mlp, norm_ple, final_norm))
    in_maps = []
    for c in range(n):
        xt = np.concatenate([x_prompt[2 * c], x_prompt[2 * c + 1], x_sample[c]], axis=0)
        pt = np.concatenate([p_prompt[:, 2 * c], p_prompt[:, 2 * c + 1], p_sample[:, c]], axis=1)
        m = dict(shared)
        m["xtok"] = np.ascontiguousarray(xt)
        m["ptok"] = np.ascontiguousarray(pt)
        in_maps.append(m)
    res = run_bass_kernel_spmd(nc, in_maps, core_ids=list(range(n)))
    y_prompt = np.empty_like(x_prompt)
    y_sample = np.empty_like(x_sample)
    for c in range(n):
        y = res.results[c]["ytok"]
        y_prompt[2 * c] = y[0:SP]
        y_prompt[2 * c + 1] = y[SP:2 * SP]
        y_sample[c] = y[2 * SP:2 * SP + SS_]
    return (y_prompt, y_sample)
```

```python
import math
from contextlib import ExitStack

import numpy as np
import concourse.bass as bass
import concourse.mybir as mybir
from concourse.bass_utils import run_bass_kernel_spmd

F32 = mybir.dt.float32
BF16 = mybir.dt.bfloat16
ALU = mybir.AluOpType
AF = mybir.ActivationFunctionType

D = 1024
KC = 8
NH = 16
DEPTH = 4
PLE = 256
DFF = 4096
EPS = 1e-6
OWN = 1024
NEG = -30000.0
REG = 256
EPOCH = 24000
NSLOT = 4
SLOT_BYTES = 8192
LOOKAHEAD = 2


def pieces(lo, hi, step):
    out = []
    a = lo
    while a < hi:
        b = min(hi, (a // step + 1) * step)
        out.append((a, b - a))
        a = b
    return out


def epieces(lo, hi):
    n = hi - lo
    k = (n + 511) // 512
    out = []
    a = lo
    for i in range(k):
        sz = n // k + (1 if i < n % k else 0)
        out.append((a, sz))
        a += sz
    return out


class V:
    __slots__ = ("ap", "regs")

    def __init__(self, ap, regs):
        self.ap = ap
        self.regs = regs


class Buf:
    def __init__(self, base_ap, space_id, off_bytes, shape, es):
        self.space = space_id << 24
        self.off = off_bytes
        self.shape = tuple(shape)
        self.es = es
        self.ap = base_ap
        strides = []
        s = 1
        for d in reversed(self.shape):
            strides.append(s)
            s *= d
        self.strides = tuple(reversed(strides))

    def v(self, *idx, p=None):
        sl = [slice(None) if p is None else slice(p[0], p[1])]
        runs = [0]
        nd = len(self.shape)
        for d, ix in enumerate(idx):
            st = self.strides[d]
            if isinstance(ix, tuple):
                lo, hi = ix
                sl.append(slice(lo, hi))
            else:
                lo, hi = ix, ix + 1
                sl.append(ix if d < nd - 1 else slice(ix, ix + 1))
            if d < nd - 1:
                runs = [r + i * st for r in runs for i in range(lo, hi)]
            else:
                last_lo, last_hi = lo, hi
        regs = set()
        for r in runs:
            b0 = (self.off + (r + last_lo) * self.es) // REG
            b1 = (self.off + (r + last_hi) * self.es - 1) // REG
            for b in range(b0, b1 + 1):
                regs.add(self.space | b)
        return V(self.ap[tuple(sl)], regs)


class Prog:
    ENGS = ("pe", "act", "dve", "pool", "sp")

    def __init__(self):
        self.ops = []
        self.last_w = {}
        self.readers = {}
        self.dma_count = {}

    def add(self, eng, fn, reads, writes, dma_key=None):
        raw = set()
        oth = set()
        lw = self.last_w
        rd = self.readers
        toks_w = set()
        toks_r = set()
        for v in writes:
            for r in v.regs:
                if r >> 24:
                    (toks_w if eng == "pe" else toks_r).add(((r >> 24) << 24) | 0xFFFFFF)
        if eng != "pe":
            for v in reads:
                for r in v.regs:
                    if r >> 24:
                        toks_r.add(((r >> 24) << 24) | 0xFFFFFF)
        if toks_w or toks_r:
            reads = list(reads) + [V(None, toks_r)]
            writes = list(writes) + [V(None, toks_w)]
        for v in reads:
            for r in v.regs:
                w = lw.get(r)
                if w is not None:
                    raw.add(w)
        oid = len(self.ops)
        for v in writes:
            for r in v.regs:
                w = lw.get(r)
                if w is not None:
                    oth.add(w)
                l = rd.get(r)
                if l:
                    oth.update(l)
                lw[r] = oid
                rd[r] = []
        for v in reads:
            for r in v.regs:
                l = rd.get(r)
                if l is None:
                    rd[r] = [oid]
                elif not l or l[-1] != oid:
                    l.append(oid)
        dval = None
        if dma_key is not None:
            dval = self.dma_count.get(dma_key, 0) + 16
            self.dma_count[dma_key] = dval
        self.ops.append([eng, fn, raw, oth, dma_key, dval, False, None])
        return oid

    def mm(self, out, lhsT, rhs, start, stop):
        self.add("pe", lambda e, o=out.ap, l=lhsT.ap, r=rhs.ap, s=start, t=stop: e.matmul(o, l, r, start=s, stop=t),
                 [lhsT, rhs], [out])

    def tr(self, out, in_, ident):
        self.add("pe", lambda e, o=out.ap, i=in_.ap, d=ident.ap: e.transpose(o, i, d), [in_, ident], [out])

    def act(self, out, in_, func, scale=1.0, bias=0.0, accum=None, eng="act"):
        if isinstance(bias, V):
            self.add(eng, lambda e, o=out.ap, i=in_.ap, f=func, s=scale, b=bias.ap: e.activation(out=o, in_=i, func=f, bias=b, scale=s),
                     [in_, bias], [out])
        elif accum is None:
            self.add(eng, lambda e, o=out.ap, i=in_.ap, f=func, s=scale, b=bias: e.activation(out=o, in_=i, func=f, bias=b, scale=s),
                     [in_], [out])
        else:
            self.add(eng, lambda e, o=out.ap, i=in_.ap, f=func, s=scale, b=bias, a=accum.ap: e.activation(out=o, in_=i, func=f, bias=b, scale=s, accum_out=a),
                     [in_], [out, accum])

    def amul(self, out, in_, c):
        if isinstance(c, V):
            self.add("act", lambda e, o=out.ap, i=in_.ap, m=c.ap: e.mul(o, i, m), [in_, c], [out])
        else:
            self.add("act", lambda e, o=out.ap, i=in_.ap, m=c: e.mul(o, i, m), [in_], [out])

    def tt(self, eng, out, in0, in1, op):
        self.add(eng, lambda e, o=out.ap, a=in0.ap, b=in1.ap, p=op: e.tensor_tensor(o, a, b, p), [in0, in1], [out])

    def stt(self, eng, out, in0, scalar, in1, op0, op1):
        if isinstance(scalar, V):
            self.add(eng, lambda e, o=out.ap, a=in0.ap, s=scalar.ap, b=in1.ap, p=op0, q=op1: e.scalar_tensor_tensor(out=o, in0=a, scalar=s, in1=b, op0=p, op1=q),
                     [in0, in1, scalar], [out])
        else:
            self.add(eng, lambda e, o=out.ap, a=in0.ap, s=scalar, b=in1.ap, p=op0, q=op1: e.scalar_tensor_tensor(out=o, in0=a, scalar=s, in1=b, op0=p, op1=q),
                     [in0, in1], [out])

    def ts(self, eng, out, in0, s1, op0):
        if isinstance(s1, V):
            self.add(eng, lambda e, o=out.ap, a=in0.ap, s=s1.ap, p=op0: e.tensor_scalar(o, a, s, None, p), [in0, s1], [out])
        else:
            self.add(eng, lambda e, o=out.ap, a=in0.ap, s=s1, p=op0: e.tensor_scalar(o, a, s, None, p), [in0], [out])

    def copy(self, eng, out, in_):
        if eng == "act":
            self.add(eng, lambda e, o=out.ap, i=in_.ap: e.copy(o, i), [in_], [out])
        else:
            self.add(eng, lambda e, o=out.ap, i=in_.ap: e.tensor_copy(o, i), [in_], [out])

    def recip(self, out, in_):
        self.add("dve", lambda e, o=out.ap, i=in_.ap: e.reciprocal(o, i), [in_], [out])

    def memset(self, eng, out, val):
        self.add(eng, lambda e, o=out.ap, c=val: e.memset(o, c), [], [out])

    def dma_in(self, eng, out, dram_ap, key):
        self.add(eng, lambda e, o=out.ap, i=dram_ap: e.dma_start(out=o, in_=i), [], [out], dma_key=key)

    def dma_out(self, eng, dram_ap, in_, key):
        self.add(eng, lambda e, o=dram_ap, i=in_.ap: e.dma_start(out=o, in_=i), [in_], [], dma_key=key)

    def emit(self, nc, es):
        ops = self.ops
        eng_of = [o[0] for o in ops]
        is_dma = [o[4] is not None for o in ops]
        need = []
        for oid, o in enumerate(ops):
            eng = o[0]
            cdeps = {}
            ddeps = {}
            for src, is_raw in ((o[2], True), (o[3], False)):
                for d in src:
                    if is_dma[d]:
                        k = ops[d][4]
                        if ops[d][5] > ddeps.get(k, 0):
                            ddeps[k] = ops[d][5]
                    else:
                        de = eng_of[d]
                        if de == eng and not is_dma[oid]:
                            if eng == "pe" or not is_raw:
                                continue
                        if d > cdeps.get(de, -1):
                            cdeps[de] = d
            for d in cdeps.values():
                ops[d][6] = True
            need.append((cdeps, ddeps))
        cnt = {e: 0 for e in self.ENGS}
        for o in ops:
            if o[6]:
                c = cnt[o[0]]
                o[7] = (c // EPOCH, c % EPOCH + 1)
                cnt[o[0]] = c + 1
        sems = {}
        for e in self.ENGS:
            for ep in range(cnt[e] // EPOCH + 1):
                sems[(e, ep)] = es.enter_context(nc.semaphore(f"s_{e}_{ep}"))
        dsem = {}
        for k in self.dma_count:
            dsem[k] = es.enter_context(nc.semaphore(f"d_{k}"))
        block = es.enter_context(nc.Block())
        per_eng = {e: [] for e in self.ENGS}
        for oid, o in enumerate(ops):
            per_eng[o[0]].append(oid)

        def run_engine(eng_name, e):
            waited = {}
            for oid in per_eng[eng_name]:
                o = ops[oid]
                cdeps, ddeps = need[oid]
                for de, d in cdeps.items():
                    sv = ops[d][7]
                    if waited.get(de, (-1, 0)) >= sv:
                        continue
                    e.wait_ge(sems[(de, sv[0])], sv[1])
                    waited[de] = sv
                for k, val in ddeps.items():
                    kk = ("dma", k, 0)
                    if waited.get(kk, 0) >= val:
                        continue
                    e.wait_ge(dsem[k], val)
                    waited[kk] = val
                ins = o[1](e)
                if o[4] is not None:
                    ins.then_inc(dsem[o[4]], 16)
                elif o[6]:
                    ins.then_inc(sems[(eng_name, o[7][0])], 1)
            if eng_name == "sp":
                for k, total in self.dma_count.items():
                    e.wait_ge(dsem[k], total)

        @block.tensor
        def _(e):
            run_engine("pe", e)

        @block.scalar
        def _(e):
            run_engine("act", e)

        @block.vector
        def _(e):
            run_engine("dve", e)

        @block.gpsimd
        def _(e):
            run_engine("pool", e)

        @block.sync
        def _(e):
            run_engine("sp", e)


def make_tiles(seq_lens):
    tiles = []
    row0 = 0
    for L in seq_lens:
        n = L // OWN
        for i in range(n):
            left = i > 0
            right = i < n - 1
            if left and right:
                hl, hr = 320, 320
            elif left:
                hl, hr = 384, 0
            elif right:
                hl, hr = 0, 384
            else:
                hl, hr = 0, 0
            T = hl + OWN + hr
            g0 = row0 + i * OWN - hl
            own = (hl, hl + OWN)
            r3 = own
            r2 = (own[0] - (2 if left else 0), own[1] + (2 if right else 0))
            r1 = (own[0] - (130 if left else 0), own[1] + (130 if right else 0))
            r0 = (own[0] - (132 if left else 0), own[1] + (132 if right else 0))
            tiles.append(dict(T=T, g0=g0, own=own, R=[r0, r1, r2, r3]))
        row0 += L
    return tiles


def build_program(seq_lens, depth=DEPTH):
    ntok = sum(seq_lens)
    nc = bass.Bass("TRN2", target_bir_lowering=False)
    dt = nc.dram_tensor
    xtok = dt("xtok", [ntok, D], F32, kind="ExternalInput").ap()
    ptok = dt("ptok", [DEPTH, ntok, PLE], F32, kind="ExternalInput").ap()
    w_qkv = dt("w_qkv", [2, D, 1536], F32, kind="ExternalInput").ap()
    w_o = dt("w_o", [2, D, D], F32, kind="ExternalInput").ap()
    w_cin = dt("w_cin", [2, D, 3 * D], F32, kind="ExternalInput").ap()
    w_cout = dt("w_cout", [2, D, D], F32, kind="ExternalInput").ap()
    w_up = dt("w_up", [DEPTH, D, DFF], F32, kind="ExternalInput").ap()
    w_down = dt("w_down", [DEPTH, DFF, D], F32, kind="ExternalInput").ap()
    w_gate = dt("w_gate", [DEPTH, D, D], F32, kind="ExternalInput").ap()
    w_proj = dt("w_proj", [DEPTH, PLE, D], F32, kind="ExternalInput").ap()
    c_ident = dt("c_ident", [128, 128], F32, kind="ExternalInput").ap()
    c_bt = dt("c_bt", [128, NH * 384], F32, kind="ExternalInput").ap()
    c_g = dt("c_g", [128, 13 * 8], F32, kind="ExternalInput").ap()
    c_cw = dt("c_cw", [128, 2 * 3 * 8], F32, kind="ExternalInput").ap()
    c_sink = dt("c_sink", [1, 2 * NH], F32, kind="ExternalInput").ap()
    c_fn = dt("c_fn", [1, D], F32, kind="ExternalInput").ap()
    ytok = dt("ytok", [ntok, D], F32, kind="ExternalOutput").ap()

    tiles = make_tiles(seq_lens)
    TM = max(t["T"] for t in tiles)
    NBM = TM // 128

    es = ExitStack()
    with es:
        lay = {}
        cur = [0]

        def alloc(name, nbytes):
            lay[name] = cur[0]
            cur[0] += (nbytes + 63) // 64 * 64

        alloc("x", KC * TM * 4)
        alloc("H", KC * TM * 2)
        alloc("W", NSLOT * SLOT_BYTES)
        alloc("BT", NH * 384 * 4)
        alloc("VA", NBM * 192 * 2)
        alloc("ident", 512)
        alloc("ones", 256)
        alloc("onesab", 512)
        alloc("ES2", 2 * 8 * 4)
        alloc("G", 13 * 8 * 4)
        alloc("CW", 48 * 4)
        alloc("ES", 32 * 4)
        alloc("GB", D * 4)
        alloc("IO", 2 * D * 4)
        alloc("PT", 2 * TM * 2)
        alloc("SQ", KC * 512 * 2)
        alloc("RS", 2 * 512 * 4)
        alloc("SS", 64)
        arena0 = cur[0]
        alloc("KTa", TM * 2)
        alloc("KTb", TM * 2)
        alloc("QT", 2 * TM * 2)
        alloc("OT", 2 * TM * 2)
        alloc("EB", 6 * 384 * 2)
        alloc("RC", 2 * 128 * 4)
        arena_end = cur[0]
        cur[0] = arena0
        alloc("Z", (TM + 2) * 4)
        alloc("Bf", TM * 4)
        alloc("Y", 4 * TM * 2)
        alloc("Ct", 2 * 512 * 4)
        alloc("AC", 2 * 512 * 4)
        arena_end = max(arena_end, cur[0])
        cur[0] = arena0
        alloc("A", 2 * 4 * 512 * 2)
        alloc("SQT", 2 * 512 * 4)
        arena_end = max(arena_end, cur[0])
        cur[0] = arena0
        alloc("Gt", 2 * 512 * 4)
        alloc("TMP", 2 * 512 * 4)
        arena_end = max(arena_end, cur[0])
        lay["PS"] = arena0 + 12288
        arena_end = max(arena_end, arena0 + 12288 + 12 * PLE * 4)
        lay["XIN"] = arena0 + 8192
        arena_end = max(arena_end, arena0 + 8192 + 4 * D * 4)
        total = arena_end
        space = es.enter_context(nc.sbuf_tensor("space", [128, total // 2], BF16))
        S = space[:, :]

        def mk(name, shape, dtype):
            esz = 4 if dtype is F32 else 2
            n = int(np.prod(shape))
            o = lay[name]
            ap = S[:, o // 2: o // 2 + n * esz // 2]
            if dtype is F32:
                ap = ap.bitcast(F32)
            if len(shape) == 2:
                ap = ap.rearrange("p (a b) -> p a b", b=shape[1])
            elif len(shape) == 3:
                ap = ap.rearrange("p (a b c) -> p a b c", b=shape[1], c=shape[2])
            return Buf(ap, 0, o, shape, esz)

        def mk_at(off, shape, dtype):
            esz = 4 if dtype is F32 else 2
            n = int(np.prod(shape))
            ap = S[:, off // 2: off // 2 + n * esz // 2]
            if dtype is F32:
                ap = ap.bitcast(F32)
            if len(shape) == 2:
                ap = ap.rearrange("p (a b) -> p a b", b=shape[1])
            return Buf(ap, 0, off, shape, esz)

        X = mk("x", (KC, TM), F32)
        H = mk("H", (KC, TM), BF16)
        BT = mk("BT", (NH, 3, 128), F32)
        BTflat = mk("BT", (NH * 384,), F32)
        VA = mk("VA", (NBM, 192), BF16)
        VAflat = mk("VA", (NBM * 192,), BF16)
        IDN = mk("ident", (128,), F32)
        ONES = mk("ones", (128,), BF16)
        ONESAB = mk("onesab", (2, 128), BF16)
        ES2 = mk("ES2", (2, 8), F32)
        G = mk("G", (13, 8), F32)
        Gflat = mk("G", (104,), F32)
        CW = mk("CW", (2, 3, 8), F32)
        CWflat = mk("CW", (48,), F32)
        ESb = mk("ES", (2, NH), F32)
        ESflat = mk("ES", (32,), F32)
        GB = mk("GB", (D,), F32)
        IO = mk("IO", (2, D), F32)
        XIN = mk("XIN", (4, D), F32)
        PSt = mk("PS", (12, PLE), F32)
        PT = mk("PT", (2, TM), BF16)
        SQ = mk("SQ", (KC, 512), BF16)
        RS = mk("RS", (2, 512), F32)
        SS = mk("SS", (16,), F32)
        KTa = mk("KTa", (TM,), BF16)
        KTb = mk("KTb", (TM,), BF16)
        QT = mk("QT", (2, TM), BF16)
        OT = mk("OT", (2, TM), BF16)
        EB = mk("EB", (6, 3, 128), BF16)
        RC = mk("RC", (2, 128), F32)
        Z = mk("Z", (TM + 2,), F32)
        Bf = mk("Bf", (TM,), F32)
        Y = mk("Y", (4, TM), BF16)
        Ct = mk("Ct", (2, 512), F32)
        AC = mk("AC", (2, 512), F32)
        Abuf = mk("A", (2, 4, 512), BF16)
        SQT = mk("SQT", (2, 512), F32)
        Gt = mk("Gt", (2, 512), F32)
        TMP = mk("TMP", (2, 512), F32)

        banks = []
        for b in range(8):
            t = es.enter_context(nc.psum_tensor(f"psb{b}", [128, 512], F32))
            banks.append(Buf(t[:, :], 1 + b, 0, (512,), 4))
        banks3 = [Buf(bk.ap.rearrange("p (a b) -> p a b", b=128), 1 + i, 0, (4, 128), 4) for i, bk in enumerate(banks)]
        mmrot = [0]

        def mmbank():
            b = mmrot[0] % 4
            mmrot[0] += 1
            return b

        P = Prog()

        P.dma_in("sp", IDN.v((0, 128)), c_ident, "c0")
        P.dma_in("sp", BTflat.v((0, NH * 384)), c_bt, "c1")
        P.dma_in("sp", Gflat.v((0, 104)), c_g, "c2")
        P.dma_in("sp", CWflat.v((0, 48)), c_cw, "c3")
        P.dma_in("sp", ESflat.v((0, 32)), c_sink.partition_broadcast(128), "c4")
        P.dma_in("sp", GB.v((0, D)), c_fn.partition_broadcast(128), "c5")
        P.memset("pool", ONES.v((0, 128)), 1.0)
        P.memset("pool", VAflat.v((0, NBM * 192)), 1.0)
        P.act(ESflat.v((0, 32)), ESflat.v((0, 32)), AF.Exp)
        P.memset("pool", ONESAB.v(0, (0, 64)), 1.0)
        P.memset("pool", ONESAB.v(0, (64, 128)), 0.0)
        P.memset("pool", ONESAB.v(1, (0, 64)), 0.0)
        P.memset("pool", ONESAB.v(1, (64, 128)), 1.0)
        for jj in range(2):
            for pp in range(2):
                rows = (0, 64) if pp == 0 else (64, 128)
                src = ESb.v(jj, (0, NH), p=rows)
                P.copy("dve", ES2.v(jj, (0, 8), p=rows), V(src.ap[:, pp:NH:2], src.regs))

        stages = []
        wslot_ctr = [0]

        def slot_buf(si, shape, elem_off=0):
            return mk_at(lay["W"] + si * SLOT_BYTES + elem_off * 2, shape, BF16)

        def wdma(si, part, view, dram_ap):
            P.dma_in("pool", view, dram_ap, f"w{si}_{part}")

        def norm_stats(a, n, k):
            P.act(SQ.v((0, KC), (0, n)), X.v((0, KC), (a, a + n)), AF.Square)
            nb = banks[7]
            for c in range(KC):
                P.mm(nb.v((0, n)), ONES.v((0, 128)), SQ.v(c, (0, n)), c == 0, c == KC - 1)
            P.act(RS.v(k, (0, n)), nb.v((0, n)), AF.Ln, scale=1.0 / D, bias=EPS)
            P.act(RS.v(k, (0, n)), RS.v(k, (0, n)), AF.Exp, scale=-0.5)

        nrm_k = [0]

        def norm_S(a, n):
            P.act(SQ.v((0, KC), (0, n)), X.v((0, KC), (a, a + n)), AF.Square)

        def norm_M(gidx, a, n):
            k = nrm_k[0] % 2
            nrm_k[0] += 1
            nb = banks[7]
            for c in range(KC):
                P.mm(nb.v((0, n)), ONES.v((0, 128)), SQ.v(c, (0, n)), c == 0, c == KC - 1)
            P.act(RS.v(k, (0, n)), nb.v((0, n)), AF.Ln, scale=1.0 / D, bias=EPS)
            P.act(RS.v(k, (0, n)), RS.v(k, (0, n)), AF.Exp, scale=-0.5)
            for c in range(KC):
                P.stt("dve", H.v(c, (a, a + n)), X.v(c, (a, a + n)), G.v(gidx, c), RS.v(k, (0, n)), ALU.mult, ALU.mult)

        def norm_to_H(gidx, lo, hi):
            for (a, n) in epieces(lo, hi):
                norm_S(a, n)
                norm_M(gidx, a, n)

        class FusedNorm:
            def __init__(self, gidx):
                self.gidx = gidx
                self.pend = None

            def after_piece(self, a, n):
                if self.pend is not None:
                    norm_M(self.gidx, *self.pend)
                norm_S(a, n)
                self.pend = (a, n)

            def finish(self):
                if self.pend is not None:
                    norm_M(self.gidx, *self.pend)
                    self.pend = None

        def load_tile(t):
            for b in range(t.get("loaded", 0), t["T"] // 128):
                load_block(t, b)

        def attn_layer(t, l):
            j = l // 2
            T = t["T"]
            qlo, qhi = t["R"][l]
            kb0 = max(0, qlo - 128) // 128
            kb1 = (min(T, qhi + 128) + 127) // 128
            klo, khi = kb0 * 128, kb1 * 128

            for g in range(4):
                def load_ag(si, g=g):
                    sb = slot_buf(si, (KC, 448))
                    wq = w_qkv[j]
                    wdma(si, 0, sb.v((0, KC), (0, 256)), wq[:, g * 256:(g + 1) * 256].rearrange("(k p) n -> p k n", p=128))
                    wdma(si, 1, sb.v((0, KC), (256, 320)), wq[:, 1024 + g * 64:1024 + (g + 1) * 64].rearrange("(k p) n -> p k n", p=128))
                    wdma(si, 2, sb.v((0, KC), (320, 384)), wq[:, 1024 + g * 64:1024 + (g + 1) * 64].rearrange("(k p) n -> p k n", p=128))
                    wdma(si, 3, sb.v((0, KC), (384, 448)), wq[:, 1280 + g * 64:1280 + (g + 1) * 64].rearrange("(k p) n -> p k n", p=128))

                def comp_ag(si, g=g):
                    sb = slot_buf(si, (KC, 448))
                    if g == 0:
                        norm_to_H(l, klo, khi)
                        P.memset("pool", KTa.v((klo, khi), p=(64, 128)), 0.0)
                        P.memset("pool", KTb.v((klo, khi), p=(0, 64)), 0.0)
                    for (a, n) in pieces(klo, khi, 512):
                        bk = mmbank()
                        for c in range(KC):
                            P.mm(banks[bk].v((0, n)), sb.v(c, (256, 384)), H.v(c, (a, a + n)), c == 0, c == KC - 1)
                        P.copy("act", KTa.v((a, a + n), p=(0, 64)), banks[bk].v((0, n), p=(0, 64)))
                        P.copy("act", KTb.v((a, a + n), p=(64, 128)), banks[bk].v((0, n), p=(64, 128)))
                    kb = kb0
                    while kb < kb1:
                        nb_ = min(4, kb1 - kb)
                        bk = mmbank()
                        for i in range(nb_):
                            for c in range(KC):
                                P.mm(banks[bk].v((i * 64, i * 64 + 64)), H.v(c, ((kb + i) * 128, (kb + i) * 128 + 128)), sb.v(c, (384, 448)), c == 0, c == KC - 1)
                        b3 = Buf(banks[bk].ap.rearrange("p (a b) -> p a b", b=64), 1 + bk, 0, (8, 64), 4)
                        P.copy("dve", VA.v((kb, kb + nb_), (64, 128)), b3.v((0, nb_), (0, 64)))
                        kb += nb_
                    for (a, n) in epieces(qlo, qhi):
                        for cl in range(2):
                            bk = mmbank()
                            for c in range(KC):
                                P.mm(banks[bk].v((0, n)), sb.v(c, (cl * 128, cl * 128 + 128)), H.v(c, (a, a + n)), c == 0, c == KC - 1)
                            P.amul(QT.v(cl, (a, a + n)), banks[bk].v((0, n)), 0.125)
                    units = [(qs, nq, cl) for (qs, nq) in pieces(qlo, qhi, 128) for cl in range(2)]

                    def geom(qs):
                        jb = qs // 128
                        ois = [oi for oi in range(3) if kb0 <= jb + oi - 1 < kb1]
                        return jb, qs % 128, ois

                    def emit_A(ui):
                        qs, nq, cl = units[ui]
                        jb, qa, ois = geom(qs)
                        for p in range(2):
                            sbk = (ui % 3) * 2 + p
                            KT = KTa if p == 0 else KTb
                            for oi in ois:
                                kbx = jb + oi - 1
                                P.mm(banks3[sbk].v(oi, (qa, qa + nq)), KT.v((kbx * 128, kbx * 128 + 128)), QT.v(cl, (qs, qs + nq)), True, True)

                    def emit_B(ui):
                        qs, nq, cl = units[ui]
                        jb, qa, ois = geom(qs)
                        o0, o1 = ois[0], ois[-1] + 1
                        for p in range(2):
                            h = g * 4 + cl * 2 + p
                            sbk = (ui % 3) * 2 + p
                            k2 = (ui * 2 + p) % 6
                            sv = banks3[sbk].v((o0, o1), (qa, qa + nq))
                            P.tt("dve", sv, sv, BT.v(h, (o0, o1), (qa, qa + nq)), ALU.add)
                            P.act(EB.v(k2, (o0, o1), (qa, qa + nq)), sv, AF.Exp)

                    def emit_C(ui):
                        qs, nq, cl = units[ui]
                        jb, qa, ois = geom(qs)
                        ob = banks3[6 + ui % 2]
                        for p in range(2):
                            k2 = (ui * 2 + p) % 6
                            for oi in ois:
                                kbx = jb + oi - 1
                                va = VA.v(kbx, (64, 192)) if p == 0 else VA.v(kbx, (0, 128))
                                P.mm(ob.v(p, (0, nq)), va, EB.v(k2, oi, (qa, qa + nq)), oi == ois[0], oi == ois[-1])
                        for p in range(2):
                            k2 = (ui * 2 + p) % 6
                            for oi in ois:
                                P.mm(ob.v(2, (0, nq)), ONESAB.v(p, (0, 128)), EB.v(k2, oi, (qa, qa + nq)),
                                     p == 0 and oi == ois[0], p == 1 and oi == ois[-1])

                    def emit_D(ui):
                        qs, nq, cl = units[ui]
                        ob = banks3[6 + ui % 2]
                        rk = ui % 2
                        P.act(RC.v(rk, (0, nq)), ob.v(2, (0, nq)), AF.Ln, bias=ES2.v(j, g * 2 + cl))
                        P.act(RC.v(rk, (0, nq)), RC.v(rk, (0, nq)), AF.Exp, scale=-1.0)

                    def emit_D2(ui):
                        qs, nq, cl = units[ui]
                        obk = 6 + ui % 2
                        ob = banks3[obk]
                        b6 = banks[obk].ap.rearrange("p (a b) -> p a b", b=128)
                        rk = ui % 2
                        for p in range(2):
                            data = (0, 64) if p == 0 else (64, 128)
                            oregs = ob.v(p, (0, nq)).regs
                            P.tt("dve", OT.v(cl, (qs, qs + nq), p=data), V(b6[data[0]:data[1], p, 0:nq], oregs), RC.v(rk, (0, nq), p=data), ALU.mult)

                    nu = len(units)
                    for u0 in range(min(3, nu)):
                        emit_A(u0)
                    for u0 in range(min(2, nu)):
                        emit_B(u0)
                    for ui in range(nu):
                        if ui + 3 < nu:
                            emit_A(ui + 3)
                        emit_C(ui)
                        emit_D(ui)
                        if ui + 2 < nu:
                            emit_B(ui + 2)
                        emit_D2(ui)

                stages.append((load_ag, comp_ag))

                def load_ao(si, g=g):
                    sb = slot_buf(si, (2, D))
                    wdma(si, 0, sb.v((0, 2), (0, D)), w_o[j][g * 256:(g + 1) * 256, :].rearrange("(k p) n -> p k n", p=128))

                def comp_ao(si, g=g):
                    sb = slot_buf(si, (2, D))
                    fn = FusedNorm(4 + l) if g == 3 else None
                    for (a, n) in epieces(qlo, qhi):
                        for m in range(KC):
                            bk = mmbank()
                            for c in range(2):
                                P.mm(banks[bk].v((0, n)), sb.v(c, (m * 128, m * 128 + 128)), OT.v(c, (a, a + n)), c == 0, c == 1)
                            P.tt("dve", X.v(m, (a, a + n)), X.v(m, (a, a + n)), banks[bk].v((0, n)), ALU.add)
                        if fn:
                            fn.after_piece(a, n)
                    if fn:
                        fn.finish()

                stages.append((load_ao, comp_ao))

        def conv_layer(t, l):
            j = l // 2
            T = t["T"]
            lo, hi = t["R"][l]
            ilo, ihi = max(0, lo - 2), min(T, hi + 2)
            for fc in range(KC):
                def load_cv(si, fc=fc):
                    sb = slot_buf(si, (KC, 384))
                    wi = w_cin[j]
                    for part in range(3):
                        wdma(si, part, sb.v((0, KC), (part * 128, part * 128 + 128)),
                             wi[:, part * D + fc * 128: part * D + (fc + 1) * 128].rearrange("(k p) n -> p k n", p=128))

                def comp_cv(si, fc=fc):
                    sb = slot_buf(si, (KC, 384))
                    if fc == 0:
                        norm_to_H(l, ilo, ihi)
                    P.memset("pool", Z.v((ilo, ilo + 1)), 0.0)
                    P.memset("pool", Z.v((ihi + 1, ihi + 2)), 0.0)
                    ck = 0
                    for (a, n) in epieces(ilo, ihi):
                        bks = []
                        for part in range(3):
                            bk = mmbank()
                            bks.append(bk)
                            for c in range(KC):
                                P.mm(banks[bk].v((0, n)), sb.v(c, (part * 128, part * 128 + 128)), H.v(c, (a, a + n)), c == 0, c == KC - 1)
                        k = ck % 2
                        ck += 1
                        P.copy("act", Bf.v((a, a + n)), banks[bks[0]].v((0, n)))
                        P.copy("act", Ct.v(k, (0, n)), banks[bks[1]].v((0, n)))
                        P.tt("dve", Z.v((a + 1, a + 1 + n)), banks[bks[2]].v((0, n)), Ct.v(k, (0, n)), ALU.mult)
                    for (a, n) in epieces(lo, hi):
                        k = ck % 2
                        ck += 1
                        acc = AC.v(k, (0, n))
                        P.amul(acc, Z.v((a, a + n)), CW.v(j, 0, fc))
                        P.stt("dve", acc, Z.v((a + 1, a + 1 + n)), CW.v(j, 1, fc), acc, ALU.mult, ALU.add)
                        P.stt("dve", acc, Z.v((a + 2, a + 2 + n)), CW.v(j, 2, fc), acc, ALU.mult, ALU.add)
                        P.tt("dve", Y.v(fc % 4, (a, a + n)), acc, Bf.v((a, a + n)), ALU.mult)

                stages.append((load_cv, comp_cv))
                if fc % 4 == 3:
                    half = fc // 4

                    def load_co(si, half=half):
                        so = slot_buf(si, (4, D))
                        wdma(si, 0, so.v((0, 4), (0, D)), w_cout[j][half * 512:(half + 1) * 512, :].rearrange("(k p) n -> p k n", p=128))

                    def comp_co(si, half=half):
                        so = slot_buf(si, (4, D))
                        fn = FusedNorm(4 + l) if half == 1 else None
                        for (a, n) in epieces(lo, hi):
                            for m in range(KC):
                                bk = mmbank()
                                for c in range(4):
                                    P.mm(banks[bk].v((0, n)), so.v(c, (m * 128, m * 128 + 128)), Y.v(c, (a, a + n)), c == 0, c == 3)
                                P.tt("dve", X.v(m, (a, a + n)), X.v(m, (a, a + n)), banks[bk].v((0, n)), ALU.add)
                            if fn:
                                fn.after_piece(a, n)
                        if fn:
                            fn.finish()

                    stages.append((load_co, comp_co))

        def mlp_layer(t, l):
            lo, hi = t["R"][l]
            for f in range(8):
                def load_u(si, f=f):
                    sb = slot_buf(si, (KC, 512))
                    wdma(si, 0, sb.v((0, KC), (0, 512)), w_up[l][:, f * 512:(f + 1) * 512].rearrange("(k p) n -> p k n", p=128))

                hold = {}

                def comp_u(si, f=f, hold=hold):
                    hold["u"] = si
                    if f == 4:
                        b0, b1 = lo // 128, (hi + 127) // 128
                        for b in range(b0, b1):
                            P.dma_in("sp", PSt.v(b - b0, (0, PLE)), ptok[l][t["g0"] + b * 128: t["g0"] + (b + 1) * 128, :], f"ps{b - b0}")

                stages.append((load_u, comp_u))

                def load_d(si, f=f):
                    sb = slot_buf(si, (4, D))
                    wdma(si, 0, sb.v((0, 4), (0, D)), w_down[l][f * 512:(f + 1) * 512, :].rearrange("(k p) n -> p k n", p=128))

                def comp_d(si, f=f, hold=hold):
                    su = slot_buf(hold["u"], (KC, 512))
                    sd = slot_buf(si, (4, D))
                    ak = 0
                    fn = FusedNorm(8 + l) if f == 7 else None
                    for (a, n) in epieces(lo, hi):
                        ka = ak % 2
                        ak += 1
                        for m in range(4):
                            bk = mmbank()
                            for c in range(KC):
                                P.mm(banks[bk].v((0, n)), su.v(c, (m * 128, m * 128 + 128)), H.v(c, (a, a + n)), c == 0, c == KC - 1)
                            k = m % 2
                            P.act(SQT.v(k, (0, n)), banks[bk].v((0, n)), AF.Square)
                            P.stt("dve", Abuf.v(ka, m, (0, n)), banks[bk].v((0, n)), 0.0, SQT.v(k, (0, n)), ALU.is_gt, ALU.mult)
                        for m in range(KC):
                            bk = mmbank()
                            for c in range(4):
                                P.mm(banks[bk].v((0, n)), sd.v(c, (m * 128, m * 128 + 128)), Abuf.v(ka, c, (0, n)), c == 0, c == 3)
                            P.tt("dve", X.v(m, (a, a + n)), X.v(m, (a, a + n)), banks[bk].v((0, n)), ALU.add)
                        if fn:
                            fn.after_piece(a, n)
                    if fn:
                        fn.finish()

                stages.append((load_d, comp_d))

        def ple_layer(t, l):
            lo, hi = t["R"][l]
            g0 = t["g0"]
            for hh in range(4):
                def load_pg(si, hh=hh):
                    sg = slot_buf(si, (KC, 256))
                    sp_ = slot_buf(si, (2, 256), elem_off=KC * 256)
                    wdma(si, 0, sg.v((0, KC), (0, 256)), w_gate[l][:, hh * 256:(hh + 1) * 256].rearrange("(k p) n -> p k n", p=128))
                    wdma(si, 1, sp_.v((0, 2), (0, 256)), w_proj[l][:, hh * 256:(hh + 1) * 256].rearrange("(k p) n -> p k n", p=128))

                def comp_pg(si, hh=hh):
                    sg = slot_buf(si, (KC, 256))
                    sp_ = slot_buf(si, (2, 256), elem_off=KC * 256)
                    if hh == 0:
                        b0, b1 = lo // 128, (hi + 127) // 128
                        for b in range(b0, b1):
                            k = b - b0
                            bk = mmbank()
                            for c in range(2):
                                P.tr(banks[bk].v((c * 128, c * 128 + 128)), PSt.v(k, (c * 128, c * 128 + 128)), IDN.v((0, 128)))
                            P.copy("act", PT.v((0, 2), (b * 128, b * 128 + 128)), banks3[bk].v((0, 2), (0, 128)))
                    kk = 0
                    for (a, n) in epieces(lo, hi):
                        for m in range(2):
                            mg = hh * 2 + m
                            bg = mmbank()
                            for c in range(KC):
                                P.mm(banks[bg].v((0, n)), sg.v(c, (m * 128, m * 128 + 128)), H.v(c, (a, a + n)), c == 0, c == KC - 1)
                            bp = mmbank()
                            for c in range(2):
                                P.mm(banks[bp].v((0, n)), sp_.v(c, (m * 128, m * 128 + 128)), PT.v(c, (a, a + n)), c == 0, c == 1)
                            k = kk % 2
                            kk += 1
                            P.act(Gt.v(k, (0, n)), banks[bg].v((0, n)), AF.Sigmoid)
                            P.tt("dve", TMP.v(k, (0, n)), banks[bp].v((0, n)), Gt.v(k, (0, n)), ALU.mult)
                            P.tt("dve", X.v(mg, (a, a + n)), X.v(mg, (a, a + n)), TMP.v(k, (0, n)), ALU.add)

                stages.append((load_pg, comp_pg))

        def load_block(t, b):
            T, g0 = t["T"], t["g0"]
            k = b % 4
            if not (b < 4 and t.get("pref")):
                P.dma_in("sp", XIN.v(k, (0, D)), xtok[g0 + b * 128: g0 + (b + 1) * 128, :], f"xin{k}")
            for half in range(2):
                bk = mmbank()
                for cc in range(4):
                    c = half * 4 + cc
                    P.tr(banks[bk].v((cc * 128, cc * 128 + 128)), XIN.v(k, (c * 128, c * 128 + 128)), IDN.v((0, 128)))
                P.copy("act" if half == 0 else "dve", X.v((half * 4, half * 4 + 4), (b * 128, b * 128 + 128)), banks3[bk].v((0, 4), (0, 128)))

        def final_tile(t, nxt=None):
            lo, hi = t["own"]
            g0 = t["g0"]
            if nxt is not None:
                for b in range(4):
                    P.dma_in("sp", XIN.v(b, (0, D)), xtok[nxt["g0"] + b * 128: nxt["g0"] + (b + 1) * 128, :], f"xin{b}")
                nxt["pref"] = True
                nxt["loaded"] = 0
            for bi in range((hi - lo) // 128):
                k = bi % 2
                ta = lo + bi * 128
                bks = []
                for half in range(2):
                    bk = mmbank()
                    bks.append(bk)
                    for cc in range(4):
                        c = half * 4 + cc
                        P.tr(banks[bk].v((cc * 128, cc * 128 + 128)), X.v(c, (ta, ta + 128)), IDN.v((0, 128)))
                P.memset("pool", SS.v((k * 4, k * 4 + 2)), 0.0)
                for half in range(2):
                    P.act(IO.v(k, (half * 512, half * 512 + 512)), banks[bks[half]].v((0, 512)), AF.Square, accum=SS.v((k * 4 + half, k * 4 + half + 1)))
                P.tt("dve", SS.v((k * 4 + 2, k * 4 + 3)), SS.v((k * 4, k * 4 + 1)), SS.v((k * 4 + 1, k * 4 + 2)), ALU.add)
                P.act(SS.v((k * 4 + 3, k * 4 + 4)), SS.v((k * 4 + 2, k * 4 + 3)), AF.Sqrt, scale=1.0 / D, bias=EPS)
                P.recip(SS.v((k * 4 + 2, k * 4 + 3)), SS.v((k * 4 + 3, k * 4 + 4)))
                for half in range(2):
                    P.stt("dve", IO.v(k, (half * 512, half * 512 + 512)), banks[bks[half]].v((0, 512)), SS.v((k * 4 + 2, k * 4 + 3)),
                          GB.v((half * 512, half * 512 + 512)), ALU.mult, ALU.mult)
                P.dma_out("sp", ytok[g0 + ta: g0 + ta + 128, :], IO.v(k, (0, D)), f"io{k}")
                if nxt is not None:
                    while nxt["loaded"] < nxt["T"] // 128 and (nxt["loaded"] + 1) * 128 <= ta + 128:
                        load_block(nxt, nxt["loaded"])
                        nxt["loaded"] += 1

        for ti, t in enumerate(tiles):
            stages.append((None, lambda si, t=t: load_tile(t)))
            for l in range(depth):
                if l % 2 == 0:
                    attn_layer(t, l)
                else:
                    conv_layer(t, l)
                mlp_layer(t, l)
                ple_layer(t, l)
            nxt = tiles[ti + 1] if ti + 1 < len(tiles) else None
            stages.append((None, lambda si, t=t, nxt=nxt: final_tile(t, nxt)))

        wstages = [i for i, s in enumerate(stages) if s[0] is not None]
        slot_of = {si: (n % NSLOT) for n, si in enumerate(wstages)}
        issued = 0

        def issue_upto(n):
            nonlocal issued
            while issued < min(n, len(wstages)):
                si = wstages[issued]
                stages[si][0](slot_of[si])
                issued += 1

        nw = 0
        for i, (ld, cp) in enumerate(stages):
            if ld is not None:
                issue_upto(nw + 1 + LOOKAHEAD)
                nw += 1
            else:
                issue_upto(nw + LOOKAHEAD)
            cp(slot_of.get(i, -1))

        P.emit(nc, es)
    return nc


_BUCKETS = None


def _bucket_table():
    global _BUCKETS
    if _BUCKETS is None:
        import jax
        import jax.numpy as jnp
        with jax.default_device(jax.devices("cpu")[0]):
            rel = jnp.arange(-255, 256)
            half = 16
            max_exact = 8
            ret = jnp.where(rel > 0, half, 0)
            n = jnp.abs(rel)
            nf = jnp.maximum(n, 1).astype(jnp.float32)
            large = max_exact + (jnp.log(nf / max_exact) / math.log(128 / max_exact) * (half - max_exact)).astype(jnp.int32)
            large = jnp.minimum(large, half - 1)
            _BUCKETS = np.asarray(ret + jnp.where(n < max_exact, n, large))
    return _BUCKETS


def _consts(rel_bias, attn_sink, conv_w, norm_mix, norm_mlp, norm_ple, final_norm):
    bucket = _bucket_table()
    k = np.arange(128)[:, None, None]
    oi = np.arange(3)[None, :, None]
    q = np.arange(128)[None, None, :]
    rel = (oi - 1) * 128 + k - q
    idx = bucket[rel + 255]
    bt = np.asarray(rel_bias, np.float32)[idx]
    bt = np.where((np.abs(rel) <= 128)[..., None], bt, np.float32(NEG))
    bt = np.ascontiguousarray(np.transpose(bt, (0, 3, 1, 2))).reshape(128, NH * 384).astype(np.float32)
    gains = np.concatenate([norm_mix, norm_mlp, norm_ple, np.asarray(final_norm)[None]], axis=0)
    g = np.ascontiguousarray(np.transpose(np.asarray(gains, np.float32).reshape(13, 8, 128), (2, 0, 1))).reshape(128, 104)
    cw = np.ascontiguousarray(np.transpose(np.asarray(conv_w, np.float32).reshape(2, 3, 8, 128), (3, 0, 1, 2))).reshape(128, 48)
    return dict(c_ident=np.eye(128, dtype=np.float32), c_bt=bt, c_g=g, c_cw=cw,
                c_sink=np.asarray(attn_sink, np.float32).reshape(1, 32),
                c_fn=np.asarray(final_norm, np.float32).reshape(1, D))


_NC_CACHE = {}


def kernel(x_prompt, x_sample, p_prompt, p_sample, rel_bias, attn_w_qkv, attn_w_o, attn_sink,
           conv_w_in, conv_w, conv_w_out, mlp_w_up, mlp_w_down, ple_w_gate, ple_w_proj,
           norm_mix, norm_mlp, norm_ple, final_norm):
    n = 8
    x_prompt = np.asarray(x_prompt, np.float32)
    x_sample = np.asarray(x_sample, np.float32)
    p_prompt = np.asarray(p_prompt, np.float32)
    p_sample = np.asarray(p_sample, np.float32)
    SP, SS_ = x_prompt.shape[1], x_sample.shape[1]
    seq_lens = (SP, SP, SS_)
    key = seq_lens
    if key not in _NC_CACHE:
        _NC_CACHE[key] = build_program(list(seq_lens))
    nc = _NC_CACHE[key]
    shared = dict(
        w_qkv=np.ascontiguousarray(attn_w_qkv, np.float32), w_o=np.ascontiguousarray(attn_w_o, np.float32),
        w_cin=np.ascontiguousarray(conv_w_in, np.float32), w_cout=np.ascontiguousarray(conv_w_out, np.float32),
        w_up=np.ascontiguousarray(mlp_w_up, np.float32), w_down=np.ascontiguousarray(mlp_w_down, np.float32),
        w_gate=np.ascontiguousarray(ple_w_gate, np.float32), w_proj=np.ascontiguousarray(ple_w_proj, np.float32),
    )
    shared.update(_consts(rel_bias, attn_sink, conv_w, norm_mix, norm_mlp, norm_ple, final_norm))
    in_maps = []
    for c in range(n):
        xt = np.concatenate([x_prompt[2 * c], x_prompt[2 * c + 1], x_sample[c]], axis=0)
        pt = np.concatenate([p_prompt[:, 2 * c], p_prompt[:, 2 * c + 1], p_sample[:, c]], axis=1)
        m = dict(shared)
        m["xtok"] = np.ascontiguousarray(xt)
        m["ptok"] = np.ascontiguousarray(pt)
        in_maps.append(m)
    res = run_bass_kernel_spmd(nc, in_maps, core_ids=list(range(n)))
    y_prompt = np.empty_like(x_prompt)
    y_sample = np.empty_like(x_sample)
    for c in range(n):
        y = res.results[c]["ytok"]
        y_prompt[2 * c] = y[0:SP]
        y_prompt[2 * c + 1] = y[SP:2 * SP]
        y_sample[c] = y[2 * SP:2 * SP + SS_]
    return (y_prompt, y_sample)
```
